# Optimizing a Trainium2 kernel written in Bass

```python
import jax, jax.numpy as jnp
from jax import lax
import numpy as np

D_MODEL = 1024
BATCH = 2
SEQ = 8192
DEPTH = 2

N_A_LAYERS = DEPTH // 2
N_B_LAYERS = DEPTH - N_A_LAYERS
D_FF = 2816
RMS_EPS = 1e-6
S5_GROUP = 16
S5_GROUPS = D_MODEL // S5_GROUP
S5_STATE = 64
S5_DT_MIN = 1e-3
S5_DT_MAX = 1e-1
N_HEADS = 8
HEAD_DIM = D_MODEL // N_HEADS
N_KV_HEADS = 2
KV_GROUP = N_HEADS // N_KV_HEADS
ROPE_DIM = HEAD_DIM // 4
ROPE_THETA = 500000.0
MOBA_BLOCK = 256
MOBA_TOPK = 3
Q_BLOCK = 64
NEG_INF = -1e30

kernel_name = "yoco_s5_moba_macaron"


def rmsnorm(x, g):
    xf = x.astype(jnp.float32)
    y = xf * lax.rsqrt(jnp.mean(xf * xf, axis=-1, keepdims=True) + RMS_EPS)
    return (y * g.astype(jnp.float32)).astype(x.dtype)


def swiglu_ffn(h, w_in, w_out):
    gate, up = jnp.split(h @ w_in, 2, axis=-1)
    return (jax.nn.silu(gate) * up) @ w_out


def partial_rope(x):
    s = x.shape[1]
    pos = jnp.arange(s, dtype=jnp.float32)
    inv_freq = ROPE_THETA ** (-jnp.arange(0, ROPE_DIM, 2, dtype=jnp.float32) / ROPE_DIM)
    ang = pos[:, None] * inv_freq[None, :]
    cos = jnp.cos(ang)[None, :, None, :]
    sin = jnp.sin(ang)[None, :, None, :]
    xr = x[..., :ROPE_DIM].astype(jnp.float32)
    x1, x2 = xr[..., :ROPE_DIM // 2], xr[..., ROPE_DIM // 2:]
    rot = jnp.concatenate([x1 * cos - x2 * sin, x2 * cos + x1 * sin], axis=-1).astype(x.dtype)
    return jnp.concatenate([rot, x[..., ROPE_DIM:]], axis=-1)


def s5_mixer(u, a_re, a_im, log_step, b_re, b_im, c_re, c_im, d_skip, w_glu):
    f32 = jnp.float32
    bsz, s, _ = u.shape
    uf = u.astype(f32).reshape(bsz, s, S5_GROUPS, S5_GROUP)
    dt = jnp.exp(log_step.astype(f32))[:, None]
    lr, li = a_re.astype(f32), a_im.astype(f32)
    mag = jnp.exp(lr * dt)
    abar_re = mag * jnp.cos(li * dt)
    abar_im = mag * jnp.sin(li * dt)
    nr, ni = abar_re - 1.0, abar_im
    den = lr * lr + li * li
    coef_re = (nr * lr + ni * li) / den
    coef_im = (ni * lr - nr * li) / den
    bu_re = jnp.einsum('bsgc,gpc->bsgp', uf, b_re.astype(f32))
    bu_im = jnp.einsum('bsgc,gpc->bsgp', uf, b_im.astype(f32))
    x_re = coef_re * bu_re - coef_im * bu_im
    x_im = coef_re * bu_im + coef_im * bu_re
    a_re_t = jnp.broadcast_to(abar_re, (1, s) + abar_re.shape)
    a_im_t = jnp.broadcast_to(abar_im, (1, s) + abar_im.shape)

    def combine(left, right):
        ar1, ai1, br1, bi1 = left
        ar2, ai2, br2, bi2 = right
        return (ar2 * ar1 - ai2 * ai1,
                ar2 * ai1 + ai2 * ar1,
                ar2 * br1 - ai2 * bi1 + br2,
                ar2 * bi1 + ai2 * br1 + bi2)

    _, _, h_re, h_im = lax.associative_scan(combine, (a_re_t, a_im_t, x_re, x_im), axis=1)
    y = (jnp.einsum('bsgp,gcp->bsgc', h_re, c_re.astype(f32))
         - jnp.einsum('bsgp,gcp->bsgc', h_im, c_im.astype(f32)))
    y = y.reshape(bsz, s, D_MODEL) + d_skip.astype(f32) * u.astype(f32)
    y = jax.nn.gelu(y).astype(u.dtype)
    val, gate = jnp.split(y @ w_glu, 2, axis=-1)
    return val * jax.nn.sigmoid(gate)


def shared_kv(x, g, w_k, w_v):
    h = rmsnorm(x, g)
    bsz, s, _ = x.shape
    k = partial_rope((h @ w_k).reshape(bsz, s, N_KV_HEADS, HEAD_DIM))
    v = (h @ w_v).reshape(bsz, s, N_KV_HEADS, HEAD_DIM)
    n_blk = -(-s // MOBA_BLOCK)
    pad = n_blk * MOBA_BLOCK - s
    k = jnp.pad(k, ((0, 0), (0, pad), (0, 0), (0, 0)))
    v = jnp.pad(v, ((0, 0), (0, pad), (0, 0), (0, 0)))
    k_blocks = k.reshape(bsz, n_blk, MOBA_BLOCK, N_KV_HEADS, HEAD_DIM).transpose(0, 3, 1, 2, 4)
    v_blocks = v.reshape(bsz, n_blk, MOBA_BLOCK, N_KV_HEADS, HEAD_DIM).transpose(0, 3, 1, 2, 4)
    k_mean = jnp.mean(k_blocks.astype(jnp.float32), axis=3).astype(k.dtype)
    return k_blocks, v_blocks, k_mean


def moba_mixer(h, w_q, w_o, k_blocks, v_blocks, k_mean):
    bsz, s, _ = h.shape
    n_blk = k_blocks.shape[2]
    top_k = min(MOBA_TOPK, n_blk)
    n_qblk = s // Q_BLOCK
    q = partial_rope((h @ w_q).reshape(bsz, s, N_HEADS, HEAD_DIM)) * (HEAD_DIM ** -0.5)
    q = q.reshape(bsz, n_qblk, Q_BLOCK, N_KV_HEADS, KV_GROUP, HEAD_DIM).transpose(1, 0, 3, 4, 2, 5)
    b_idx = jnp.arange(bsz)[:, None, None, None, None]
    h_idx = jnp.arange(N_KV_HEADS)[None, :, None, None, None]
    key_off = jnp.arange(MOBA_BLOCK)
    blk_ids = jnp.arange(n_blk)
    sel_slot = jnp.arange(top_k)

    def attend_block(args):
        qb, i = args
        q_pos = i * Q_BLOCK + jnp.arange(Q_BLOCK)
        own = (i * Q_BLOCK) // MOBA_BLOCK
        k_own = lax.dynamic_index_in_dim(k_blocks, own, axis=2, keepdims=False)
        v_own = lax.dynamic_index_in_dim(v_blocks, own, axis=2, keepdims=False)
        s_own = jnp.einsum('bhgqd,bhkd->bhgqk', qb, k_own).astype(jnp.float32)
        causal = (own * MOBA_BLOCK + key_off)[None, :] <= q_pos[:, None]
        s_own = jnp.where(causal, s_own, NEG_INF)
        gate = jnp.einsum('bhgqd,bhnd->bhgqn', qb, k_mean).astype(jnp.float32)
        gate = jnp.where(blk_ids < own, gate, NEG_INF)
        _, idx = lax.top_k(gate, top_k)
        k_sel = k_blocks[b_idx, h_idx, idx]
        v_sel = v_blocks[b_idx, h_idx, idx]
        s_sel = jnp.einsum('bhgqd,bhgqjkd->bhgqjk', qb, k_sel).astype(jnp.float32)
        s_sel = jnp.where((sel_slot < own)[:, None], s_sel, NEG_INF)
        scores = jnp.concatenate(
            [s_own, s_sel.reshape(s_sel.shape[:4] + (top_k * MOBA_BLOCK,))], axis=-1)
        p = jax.nn.softmax(scores, axis=-1).astype(v_blocks.dtype)
        p_own = p[..., :MOBA_BLOCK]
        p_sel = p[..., MOBA_BLOCK:].reshape(s_sel.shape)
        return (jnp.einsum('bhgqk,bhkd->bhgqd', p_own, v_own)
                + jnp.einsum('bhgqjk,bhgqjkd->bhgqd', p_sel, v_sel))

    out = lax.map(attend_block, (q, jnp.arange(n_qblk)))
    out = out.transpose(1, 0, 4, 2, 3, 5).reshape(bsz, s, N_HEADS * HEAD_DIM)
    return out @ w_o


def setup_inputs(seed: int = 0) -> dict:
    key = jax.random.key(seed)
    ks = jax.random.split(key, 24)
    f32 = jnp.float32

    def nrm(k, shape, scale):
        return jax.random.normal(k, shape, f32) * scale

    x = jax.random.normal(ks[0], (BATCH, SEQ, D_MODEL), f32)
    norm_g = 1.0 + nrm(ks[1], (DEPTH, 3, D_MODEL), 0.02)
    ffn_w_in = nrm(ks[2], (DEPTH, 2, D_MODEL, 2 * D_FF), D_MODEL ** -0.5)
    ffn_w_out = nrm(ks[3], (DEPTH, 2, D_FF, D_MODEL), D_FF ** -0.5)
    n_idx = jnp.arange(S5_STATE, dtype=f32)
    s5_a_re = -0.5 + nrm(ks[4], (N_A_LAYERS, S5_GROUPS, S5_STATE), 0.01)
    s5_a_im = jnp.pi * n_idx + nrm(ks[5], (N_A_LAYERS, S5_GROUPS, S5_STATE), 0.01)
    s5_log_step = jax.random.uniform(ks[6], (N_A_LAYERS, S5_GROUPS), f32,
                                     float(np.log(S5_DT_MIN)), float(np.log(S5_DT_MAX)))
    b_scale = (2.0 * S5_GROUP) ** -0.5
    s5_b_re = nrm(ks[7], (N_A_LAYERS, S5_GROUPS, S5_STATE, S5_GROUP), b_scale)
    s5_b_im = nrm(ks[8], (N_A_LAYERS, S5_GROUPS, S5_STATE, S5_GROUP), b_scale)
    c_scale = S5_STATE ** -0.5
    s5_c_re = nrm(ks[9], (N_A_LAYERS, S5_GROUPS, S5_GROUP, S5_STATE), c_scale)
    s5_c_im = nrm(ks[10], (N_A_LAYERS, S5_GROUPS, S5_GROUP, S5_STATE), c_scale)
    s5_d = nrm(ks[11], (N_A_LAYERS, D_MODEL), 1.0)
    s5_w_glu = nrm(ks[12], (N_A_LAYERS, D_MODEL, 2 * D_MODEL), D_MODEL ** -0.5)
    kv_norm_g = 1.0 + nrm(ks[13], (D_MODEL,), 0.02)
    w_k = nrm(ks[14], (D_MODEL, N_KV_HEADS * HEAD_DIM), D_MODEL ** -0.5)
    w_v = nrm(ks[15], (D_MODEL, N_KV_HEADS * HEAD_DIM), D_MODEL ** -0.5)
    w_q = nrm(ks[16], (N_B_LAYERS, D_MODEL, N_HEADS * HEAD_DIM), D_MODEL ** -0.5)
    w_o = nrm(ks[17], (N_B_LAYERS, N_HEADS * HEAD_DIM, D_MODEL), (N_HEADS * HEAD_DIM) ** -0.5)
    final_g = 1.0 + nrm(ks[18], (D_MODEL,), 0.02)
    return {"x": x, "norm_g": norm_g, "ffn_w_in": ffn_w_in, "ffn_w_out": ffn_w_out,
            "s5_a_re": s5_a_re, "s5_a_im": s5_a_im, "s5_log_step": s5_log_step,
            "s5_b_re": s5_b_re, "s5_b_im": s5_b_im, "s5_c_re": s5_c_re, "s5_c_im": s5_c_im,
            "s5_d": s5_d, "s5_w_glu": s5_w_glu, "kv_norm_g": kv_norm_g, "w_k": w_k, "w_v": w_v,
            "w_q": w_q, "w_o": w_o, "final_g": final_g}


def reference(x, norm_g, ffn_w_in, ffn_w_out, s5_a_re, s5_a_im, s5_log_step,
              s5_b_re, s5_b_im, s5_c_re, s5_c_im, s5_d, s5_w_glu, kv_norm_g, w_k, w_v,
              w_q, w_o, final_g):
    kv = None
    for layer in range(DEPTH):
        if layer == N_A_LAYERS:
            kv = shared_kv(x, kv_norm_g, w_k, w_v)
        x = x + 0.5 * swiglu_ffn(rmsnorm(x, norm_g[layer, 0]), ffn_w_in[layer, 0], ffn_w_out[layer, 0])
        h = rmsnorm(x, norm_g[layer, 1])
        if layer < N_A_LAYERS:
            j = layer
            x = x + s5_mixer(h, s5_a_re[j], s5_a_im[j], s5_log_step[j], s5_b_re[j], s5_b_im[j],
                             s5_c_re[j], s5_c_im[j], s5_d[j], s5_w_glu[j])
        else:
            j = layer - N_A_LAYERS
            x = x + moba_mixer(h, w_q[j], w_o[j], kv[0], kv[1], kv[2])
        x = x + 0.5 * swiglu_ffn(rmsnorm(x, norm_g[layer, 2]), ffn_w_in[layer, 1], ffn_w_out[layer, 1])
    return rmsnorm(x, final_g)
```

```python
import numpy as np
import concourse.bass as bass
import concourse.mybir as mybir
from concourse.bass_utils import run_bass_kernel_spmd

F32 = mybir.dt.float32
BF16 = mybir.dt.bfloat16
I32 = mybir.dt.int32
AF = mybir.ActivationFunctionType
ALU = mybir.AluOpType

D = 1024
NT = 2048
TB = 512
NTB = NT // TB
DFF = 2816
NCH = 8
NFF = 22
EPS = 1e-6
N_CORES = 8


def zz_block(r, i):
    return 8 * (i // 2) + (r if i % 2 == 0 else 7 - r)


class Op:
    __slots__ = ("eng", "idx", "fn", "deps", "dma", "marked", "sig", "inc")

    def __init__(self, eng, idx, fn, deps, dma):
        self.eng, self.idx, self.fn, self.deps, self.dma = eng, idx, fn, deps, dma
        self.marked = False
        self.sig = None
        self.inc = 16


class Sched:
    ENGS = ("pe", "act", "dve", "pool", "sp")

    def __init__(self, nc):
        self.nc = nc
        self.ops = {e: [] for e in self.ENGS}
        self.res = {}
        self.dma_cnt = {}

    @staticmethod
    def _merge(dct, dep):
        k = (dep[0], dep[1])
        if k not in dct or dct[k][2] < dep[2]:
            dct[k] = dep

    def op(self, eng, fn, r=(), w=(), dma=None, inc=16):
        lst = self.ops[eng]
        idx = len(lst)
        pr = [x for x in r if len(x) == 3 and x.startswith("ps")]
        if pr:
            r = [x for x in r if x not in pr]
            w = list(w) + pr
        if dma is not None:
            cnt = self.dma_cnt.get(dma, 0) + inc
            self.dma_cnt[dma] = cnt
            me = ("d", dma, cnt)
        else:
            me = ("c", eng, idx)
        deps = {}
        for x in r:
            st = self.res.setdefault(x, [None, {}])
            if st[0] is not None:
                self._merge(deps, st[0])
        for x in w:
            st = self.res.setdefault(x, [None, {}])
            if st[0] is not None:
                self._merge(deps, st[0])
            for dp in st[1].values():
                self._merge(deps, dp)
        for x in r:
            self._merge(self.res[x][1], me)
        for x in w:
            self.res[x][0] = me
            self.res[x][1] = {}
        if me[0] == "c" and eng == "pe":
            deps.pop(("c", "pe"), None)
        assert fn is not None or not w, "no-instruction ops cannot produce resources"
        o = Op(eng, idx, fn, list(deps.values()), dma)
        o.inc = inc
        if dma is not None:
            o.sig = me
        lst.append(o)
        return o

    def emit(self):
        nc = self.nc
        for e in self.ENGS:
            for o in self.ops[e]:
                for d in o.deps:
                    if d[0] == "c":
                        self.ops[d[1]][d[2]].marked = True
        sems = {}
        for e in self.ENGS:
            sems[("c", e)] = nc.alloc_semaphore(name="sem_" + e)
            cnt = 0
            for o in self.ops[e]:
                if o.dma is None and o.marked:
                    cnt += 1
                    o.sig = ("c", e, cnt)
        for k in self.dma_cnt:
            sems[("d", k)] = nc.alloc_semaphore(name="dsem_" + k.replace(".", "_"))
        self.nsem = len(sems)
        engobj = {"pe": "tensor", "act": "scalar", "dve": "vector", "pool": "gpsimd", "sp": "sync"}

        def replay(ename, e):
            waited = {}
            for o in self.ops[ename]:
                for d in o.deps:
                    if d[0] == "c":
                        tgt = self.ops[d[1]][d[2]].sig
                    else:
                        tgt = d
                    key = (tgt[0], tgt[1])
                    if waited.get(key, 0) < tgt[2]:
                        e.wait_ge(sems[key], tgt[2])
                        waited[key] = tgt[2]
                if o.fn is None:
                    continue
                ins = o.fn(e)
                if o.dma is not None:
                    ins.then_inc(sems[("d", o.dma)], o.inc)
                elif o.marked:
                    ins.then_inc(sems[("c", ename)], 1)

        with nc.Block() as block:
            @block.tensor
            def _(e):
                replay("pe", e)

            @block.scalar
            def _(e):
                replay("act", e)

            @block.vector
            def _(e):
                replay("dve", e)

            @block.gpsimd
            def _(e):
                replay("pool", e)

            @block.sync
            def _(e):
                replay("sp", e)


class Builder:
    def __init__(self, cfg):
        self.cfg = cfg
        self.nc = bass.Bass("TRN2", target_bir_lowering=False)
        self.S = Sched(self.nc)
        self.sb_bytes = 0
        self.ps_rr = 0
        self.din = {}
        self.dbg_keys = []

    def sb(self, name, shape, dt):
        n = 1
        for s in shape[1:]:
            n *= s
        self.sb_bytes += n * (4 if dt in (F32, I32) else 2)
        return self.nc.alloc_sbuf_tensor(name, list(shape), dt)

    def carve(self, shape, dt):
        n = 1
        for v in shape[1:]:
            n *= v
        esz = 4 if dt in (F32, I32) else 2
        off = self.arena_off
        nb = (n * esz + 63) // 64 * 64
        assert off + nb <= self.ARENA_BYTES, ("arena overflow", off, nb)
        self.arena_off = off + nb
        self.arena_max = max(self.arena_max, self.arena_off)
        v = self.arena[0:shape[0], off // 2: off // 2 + n * esz // 2]
        if dt != BF16:
            v = v.bitcast(dt)
        if len(shape) == 3:
            v = v.rearrange("p (a b) -> p a b", a=shape[1])
        elif len(shape) == 4:
            v = v.rearrange("p (a b c) -> p a b c", a=shape[1], b=shape[2])
        elif len(shape) == 5:
            v = v.rearrange("p (a b c d) -> p a b c d", a=shape[1], b=shape[2], c=shape[3])
        return v

    def phase(self, tag):
        S = self.S
        deps = {}
        for e in S.ENGS:
            if S.ops[e]:
                for o in reversed(S.ops[e]):
                    if o.dma is None and o.fn is not None:
                        deps[("c", e)] = ("c", e, o.idx)
                        break
        for k, cnt in S.dma_cnt.items():
            deps[("d", k)] = ("d", k, cnt)
        for e in S.ENGS:
            dl = [d for kk, d in deps.items() if not (kk == ("c", e) and e == "pe")]
            o = Op(e, len(S.ops[e]), None, dl, None)
            S.ops[e].append(o)
        self.arena_off = 0
        self.ptag = tag

    def dram_in(self, name, shape, dt=F32):
        t = self.nc.dram_tensor(name, list(shape), dt, kind="ExternalInput")
        self.din[name] = t
        return t.ap()

    def mm(self, out, lhsT, rhs, start, stop, r, w):
        return self.S.op("pe", lambda e: e.matmul(out, lhsT, rhs, start=start, stop=stop), r=r, w=w)

    def tr(self, out, in_, ident, r, w):
        return self.S.op("pe", lambda e: e.transpose(out, in_, ident), r=r, w=w)

    def act(self, out, in_, func, r, w, bias=None, scale=None, accum_out=None, eng="act"):
        kw = {}
        if bias is not None:
            kw["bias"] = bias
        if scale is not None:
            kw["scale"] = scale
        if accum_out is not None:
            kw["accum_out"] = accum_out
        return self.S.op(eng, lambda e: e.activation(out, in_, func, **kw), r=r, w=w)

    def tt(self, out, in0, in1, op, r, w, eng="dve"):
        return self.S.op(eng, lambda e: e.tensor_tensor(out, in0, in1, op), r=r, w=w)

    def ts(self, out, in0, s1, s2, op0, op1, r, w, eng="dve"):
        if op1 is None:
            return self.S.op(eng, lambda e: e.tensor_scalar(out, in0, s1, None, op0), r=r, w=w)
        return self.S.op(eng, lambda e: e.tensor_scalar(out, in0, s1, s2, op0, op1), r=r, w=w)

    def stt(self, out, in0, scalar, in1, op0, op1, r, w, eng="dve"):
        return self.S.op(eng, lambda e: e.scalar_tensor_tensor(out, in0, scalar, in1, op0, op1), r=r, w=w)

    def cp(self, out, in_, r, w, eng="dve"):
        if eng == "act":
            return self.S.op(eng, lambda e: e.copy(out, in_), r=r, w=w)
        return self.S.op(eng, lambda e: e.tensor_copy(out, in_), r=r, w=w)

    def memset(self, ap, val, w, eng="dve", r=()):
        return self.S.op(eng, lambda e: e.memset(ap, val), r=list(r), w=w)

    @staticmethod
    def bc(ap, axis, n):
        v = ap.unsqueeze(axis)
        shp = list(v.shape)
        shp[axis] = n
        return v.to_broadcast(shp)

    def carve_at(self, off, shape, dt):
        save = self.arena_off
        self.arena_off = off
        v = self.carve(shape, dt)
        end = self.arena_off
        self.arena_off = save
        return v, end

    def barrier(self):
        off = self.arena_off
        self.phase("b")
        self.arena_off = off

    def dbg_out(self, name, ap, shape, r):
        if name not in self.cfg.get("dbg", ()):
            return
        t = self.nc.dram_tensor("dbg_" + name, list(shape), F32, kind="ExternalOutput").ap()
        self.dma(t, ap, "dbg_" + name, r=r, w=["dbg_" + name], eng="pool")
        self.dbg_keys.append("dbg_" + name)

    def allgather(self, out_t, in_t, key, r, w):
        groups = [[0, 1, 2, 3], [4, 5, 6, 7]]
        return self.S.op("pool", lambda e: e.collective_compute(
            "AllGather", ALU.bypass, replica_groups=groups, ins=[in_t.ap().opt()], outs=[out_t.ap().opt()]),
            r=r, w=w, dma=key, inc=1)

    def dma(self, out, in_, key, r, w, eng="sp", **kw):
        return self.S.op(eng, lambda e: e.dma_start(out=out, in_=in_, **kw), r=r, w=w, dma=key)

    def setup_common(self):
        nc = self.nc
        self.x_in = self.dram_in("x", [NT, D])
        self.norm_g = self.dram_in("norm_g", [2, 3, D])
        sk = self.cfg.get("skip", ())
        if not (1 in sk and 3 in sk and self.cfg.get("stages", 99) < 5):
            self.w_in = self.dram_in("ffn_w_in", [2, 2, D, 2 * DFF])
            self.w_out = self.dram_in("ffn_w_out", [2, 2, DFF, D])
        self.final_g = self.dram_in("final_g", [D])
        self.out = nc.dram_tensor("out", [NT, D], F32, kind="ExternalOutput").ap()

        self.xT = self.sb("xT", [128, NCH, NT], F32)
        self.ARENA_BYTES = 105 * 1024
        self.arena = self.sb("arena", [128, self.ARENA_BYTES // 2], BF16)
        self.arena_off = 0
        self.arena_max = 0
        self.ident = self.sb("ident", [128, 128], F32)
        self.onesm = self.sb("onesm", [128, 128], BF16)
        self.iot = self.sb("iot", [128, 128], F32)
        self.ps = [nc.alloc_psum_tensor("ps%d" % i, [128, 512], F32) for i in range(8)]
        self.gcol = self.sb("gcol", [128, 9, NCH], F32)
        self.grow2 = self.sb("grow2", [8, 128], F32)
        self.grow = self.sb("grow", [64, 128], F32)
        self.wbuf = self.sb("wbuf", [128, 18432], BF16)
        self.win_s = [self.wbuf[:, i * 4096:(i + 1) * 4096].rearrange("p (k g n) -> p k g n", k=NCH, g=2)
                      for i in range(3)]
        self.wout_s = [self.wbuf[:, 12288 + i * 1536:12288 + (i + 1) * 1536].rearrange("p (j n) -> p j n", j=12)
                       for i in range(4)]
        self.identb2 = self.sb("identb2", [128, 128], BF16)
        self.attn_rr = 0
        self.win_rr = 0
        self.wout_rr = 0
        self.mm1_rr = 0
        self.mm2_rr = 0

        S = self.S
        iot, ident, onesm = self.iot, self.ident, self.onesm
        S.op("pool", lambda e: e.iota(iot[:, :], [[1, 128]], base=0, channel_multiplier=-1,
                                      allow_small_or_imprecise_dtypes=True), w=["iot"])
        self.ts(ident[:, :], iot[:, :], 0.0, None, ALU.is_equal, None, r=["iot"], w=["ident"])
        S.op("dve", lambda e: e.memset(onesm[:, :], 1.0 / 1024.0), w=["onesm"])
        self.cp(self.identb2[:, :], ident[:, :], r=["ident"], w=["identb2"])

        grow = self.grow
        S.op("dve", lambda e: e.memset(grow[:, :], 0.0), w=["grow"])
        ng = self.norm_g.rearrange("l n (c p) -> (l n c) p", p=128)
        self.dma(grow[0:48, :], ng, "grow", r=[], w=["grow"])
        fg = self.final_g.rearrange("(c p) -> c p", p=128)
        self.dma(grow[48:56, :], fg, "grow", r=[], w=["grow"])
        if self.cfg.get("stages", 99) >= 2 and 2 not in self.cfg.get("skip", ()):
            self.s5_setup()
            self.dma(grow[56:64, :], self.s5d[0].rearrange("(c p) -> c p", p=128), "grow", r=[], w=["grow"])
        self.tr(self.ps[7][:, 0:64], grow[:, :], ident[0:64, 0:64], r=["grow", "ident"], w=["ps7"])
        gc = self.gcol
        self.cp(gc[:, 0:8, :].rearrange("p n c -> p (n c)"), self.ps[7][:, 0:64], r=["ps7"], w=["gcol"])
        if self.cfg.get("stages", 99) >= 4:
            self.kvg = self.dram_in("kv_norm_g", [D])
            self.dma(self.grow2[:, :], self.kvg.rearrange("(c p) -> c p", p=128), "grow2", r=[], w=["grow2"])
            self.tr(self.ps[7][:, 64:72], self.grow2[:, :], ident[0:8, 0:8], r=["grow2", "ident"], w=["ps7"])
            self.cp(gc[:, 8, :], self.ps[7][:, 64:72], r=["ps7"], w=["gcol"])

    def load_x(self):
        self.phase("load")
        self.stage = [self.carve([128, D], F32) for i in range(2)]
        for tt in range(NT // 128):
            st = self.stage[tt % 2]
            sn = "stage%d" % (tt % 2)
            self.dma(st[:, :], self.x_in[tt * 128:(tt + 1) * 128, :], sn, r=[], w=[sn])
            for half in range(2):
                b = 5 + ((2 * tt + half) % 2)
                pn = "ps%d" % b
                for q in range(4):
                    c = half * 4 + q
                    self.tr(self.ps[b][:, q * 128:(q + 1) * 128], st[:, c * 128:(c + 1) * 128],
                            self.ident[:, :], r=[sn, "ident"], w=[pn])
                dst = self.xT[:, half * 4:half * 4 + 4, tt * 128:(tt + 1) * 128]
                src = self.ps[b][:, :].rearrange("p (q t) -> p q t", q=4)
                wr = ["xT.%d.%d" % (half * 4 + q, tt // 4) for q in range(4)]
                self.cp(dst, src, r=[pn], w=wr, eng=("act" if half == 0 else "dve"))

    def store_x(self, final_norm):
        self.phase("store")
        self.stage = [self.carve([128, D], F32) for i in range(2)]
        if final_norm:
            self.gfin = self.carve([128, D], F32)
            self.dma(self.gfin[:, :], self.final_g.partition_broadcast(128), "gfin", r=[], w=["gfin"])
            self.ssum = self.carve([128, 2], F32)
            self.junk = self.carve([128, D], BF16)
        for tt in range(NT // 128):
            st = self.stage[tt % 2]
            sn = "stage%d" % (tt % 2)
            for half in range(2):
                b = 5 + ((2 * tt + half) % 2)
                pn = "ps%d" % b
                for q in range(4):
                    c = half * 4 + q
                    self.tr(self.ps[b][:, q * 128:(q + 1) * 128], self.xT[:, c, tt * 128:(tt + 1) * 128],
                            self.ident[:, :], r=["xT.%d.%d" % (c, tt // 4), "ident"], w=[pn])
                self.cp(st[:, half * 512:(half + 1) * 512], self.ps[b][:, :], r=[pn], w=[sn],
                        eng=("act" if half == 0 else "dve"))
            if final_norm:
                ss = self.ssum
                k = tt % 2
                self.act(self.junk[:, :], st[:, :], AF.Square, r=[sn], w=["junk", "ssum%d" % k],
                         accum_out=ss[:, k:k + 1])
                self.ts(ss[:, k:k + 1], ss[:, k:k + 1], 1.0 / 1024.0, EPS, ALU.mult, ALU.add,
                        r=["ssum%d" % k], w=["ssum%d" % k])
                self.act(ss[:, k:k + 1], ss[:, k:k + 1], AF.Sqrt, r=["ssum%d" % k], w=["ssum%d" % k])
                self.S.op("dve", lambda e, o=ss[:, k:k + 1]: e.reciprocal(o, o),
                          r=["ssum%d" % k], w=["ssum%d" % k])
                self.stt(st[:, :], st[:, :], ss[:, k:k + 1], self.gfin[:, :], ALU.mult, ALU.mult,
                         r=[sn, "ssum%d" % k, "gfin"], w=[sn])
            self.dma(self.out[tt * 128:(tt + 1) * 128, :], st[:, :], "out%d" % (tt % 2), r=[sn], w=["out.%d" % (tt % 2)])
        self.S.op("sp", None, r=["out.0", "out.1"], w=[])

    def rmsnorm(self, n):
        g = self.gcol
        for t in range(NTB):
            tb = slice(t * TB, (t + 1) * TB)
            sq = self.sq
            sqn = "sq"
            for c in range(NCH):
                self.act(sq[:, c, :], self.xT[:, c, tb], AF.Square, r=["xT.%d.%d" % (c, t)], w=[sqn + ".%d" % c])
            for c in range(NCH):
                self.mm(self.ps[4][:, :], self.onesm[:, :], sq[:, c, :], c == 0, c == NCH - 1,
                        r=["onesm", sqn + ".%d" % c], w=["ps4"])
            rs = self.rstd[t % 2]
            rn = "rstd%d" % (t % 2)
            self.ts(rs[:, :], self.ps[4][:, :], EPS, None, ALU.add, None, r=["ps4"], w=[rn])
            self.act(rs[:, :], rs[:, :], AF.Sqrt, r=[rn], w=[rn])
            self.S.op("dve", lambda e, o=rs[:, :]: e.reciprocal(o, o), r=[rn], w=[rn])
            for c in range(NCH):
                self.stt(self.hT[:, c, tb], self.xT[:, c, tb], g[:, n, c:c + 1], rs[:, :], ALU.mult, ALU.mult,
                         r=["xT.%d.%d" % (c, t), "gcol", rn], w=["hT.%d.%d" % (c, t)])

    def ffn(self, l, i, n):
        self.phase("ffn%d%d" % (l, i))
        self.hT = self.carve([128, NCH, NT], BF16)
        self.actT = self.carve([128, 12, NT], BF16)
        self.sg = [self.carve([128, TB], F32) for _ in range(2)]
        self.sq = self.carve([128, NCH, TB], BF16)
        self.rstd = [self.carve([128, TB], F32) for _ in range(2)]
        self.rmsnorm(n)
        win = self.w_in[l, i].rearrange("(k p) f -> p k f", p=128)
        wout = self.w_out[l, i].rearrange("(j p) n -> p j n", p=128)
        for (j0, nj) in ((0, 12), (12, 10)):
            for gs in range(nj // 2):
                s = self.win_rr % 3
                self.win_rr += 1
                wsl = self.win_s[s]
                wn = "win%d" % s
                c0 = (j0 + 2 * gs) * 128
                self.dma(wsl[:, :, 0, :], win[:, :, c0:c0 + 256], wn + "g", r=[], w=[wn + "g"], eng="pool")
                self.dma(wsl[:, :, 1, :], win[:, :, DFF + c0:DFF + c0 + 256], wn + "u", r=[], w=[wn + "u"], eng="pool")
                for jl in range(2):
                    jj = 2 * gs + jl
                    for t in range(NTB):
                        tb = slice(t * TB, (t + 1) * TB)
                        pa = 2 * (self.mm1_rr % 2)
                        self.mm1_rr += 1
                        for gu in range(2):
                            for k in range(NCH):
                                self.mm(self.ps[pa + gu][:, :], wsl[:, k, gu, jl * 128:(jl + 1) * 128],
                                        self.hT[:, k, tb], k == 0, k == NCH - 1,
                                        r=[wn + "gu"[gu], "hT.%d.%d" % (k, t)], w=["ps%d" % (pa + gu)])
                        sg = self.sg[(pa // 2)]
                        sgn = "sg%d" % (pa // 2)
                        self.act(sg[:, :], self.ps[pa][:, :], AF.Silu, r=["ps%d" % pa], w=[sgn])
                        self.tt(self.actT[:, jj, tb], sg[:, :], self.ps[pa + 1][:, :], ALU.mult,
                                r=[sgn, "ps%d" % (pa + 1)], w=["actT.%d.%d" % (jj, t)])
            for c in range(NCH):
                s = self.wout_rr % 4
                self.wout_rr += 1
                wsl = self.wout_s[s]
                wn = "wout%d" % s
                self.dma(wsl[:, 0:nj, :], wout[:, j0:j0 + nj, c * 128:(c + 1) * 128], wn, r=[], w=[wn], eng="pool")
                for t in range(NTB):
                    tb = slice(t * TB, (t + 1) * TB)
                    pb = 4 + (self.mm2_rr % 4)
                    self.mm2_rr += 1
                    for jj in range(nj):
                        self.mm(self.ps[pb][:, :], wsl[:, jj, :], self.actT[:, jj, tb], jj == 0, jj == nj - 1,
                                r=[wn, "actT.%d.%d" % (jj, t)], w=["ps%d" % pb])
                    self.stt(self.xT[:, c, tb], self.ps[pb][:, :], 0.5, self.xT[:, c, tb], ALU.mult, ALU.add,
                             r=["ps%d" % pb, "xT.%d.%d" % (c, t)], w=["xT.%d.%d" % (c, t)])


    def s5_setup(self):
        nc, S = self.nc, self.S
        self.a_re = self.dram_in("s5_a_re", [1, 64, 64])
        self.a_im = self.dram_in("s5_a_im", [1, 64, 64])
        self.lstep = self.dram_in("s5_log_step", [1, 64])
        self.b_re = self.dram_in("s5_b_re", [1, 64, 64, 16])
        self.b_im = self.dram_in("s5_b_im", [1, 64, 64, 16])
        self.c_re = self.dram_in("s5_c_re", [1, 64, 16, 64])
        self.c_im = self.dram_in("s5_c_im", [1, 64, 16, 64])
        self.s5d = self.dram_in("s5_d", [1, D])
        self.wglu = self.dram_in("s5_w_glu", [1, D, 2 * D])

    def s5_tables(self):
        NK = 34
        self.tabre = self.carve([128, 32, NK], F32)
        self.tabim = self.carve([128, 32, NK], F32)
        self.bm = [self.carve([128, 32, 2, 16], F32) for _ in range(2)]
        self.bmb = [self.carve([128, 32, 2, 16], BF16) for _ in range(2)]
        self.identb = self.carve([128, 128], BF16)
        self.rowpar = self.carve([128, 2], F32)
        self.dcol = self.carve([128, NCH], F32)
        keep = self.arena_off
        tl = self.carve([32, 3, 128], F32)
        ls2 = self.carve([32, 2], F32)
        sc = self.carve([128, 3, 32], F32)
        kv = self.carve([128, NK], F32)
        t = [self.carve([128, 32, NK], F32) for _ in range(4)]
        ti = self.carve([128, 32, NK], I32)
        cf = self.carve([128, 8, 32], F32)
        bld = [self.carve([128, 32, 16], F32) for _ in range(2)]
        bp = [self.carve([128, 32, 16], F32) for _ in range(2)]
        pm = self.carve([128, 2], F32)
        ipar = self.carve([128, 1], I32)
        S = self.S
        self.cp(self.identb[:, :], self.ident[:, :], r=["ident"], w=["identb"])
        self.dma(tl[:, 0, :], self.a_re[0].rearrange("(q two) p -> q (two p)", two=2), "tl0", r=[], w=["tl0"])
        self.dma(tl[:, 1, :], self.a_im[0].rearrange("(q two) p -> q (two p)", two=2), "tl1", r=[], w=["tl1"])
        self.dma(ls2[:, :], self.lstep[0].rearrange("(q two) -> q two", two=2), "ls2", r=[], w=["ls2"])
        self.cp(tl[:, 2, :].rearrange("q (two p) -> q two p", two=2), self.bc(ls2[:, :], 2, 64),
                r=["ls2"], w=["tl2"])
        for i in range(3):
            self.tr(self.ps[5][:, i * 32:(i + 1) * 32], tl[:, i, :], self.ident[0:32, 0:32],
                    r=["tl%d" % i, "ident"], w=["ps5"])
        self.cp(sc[:, :, :], self.ps[5][:, 0:96].rearrange("p (a b) -> p a b", a=3), r=["ps5"], w=["sc"])
        self.act(sc[:, 2, :], sc[:, 2, :], AF.Exp, r=["sc"], w=["sc"])
        for h in range(2):
            for (dst, src, nm) in ((bld[0], self.b_re, "bld0"), (bld[1], self.b_im, "bld1")):
                self.dma(dst[64 * h:64 * h + 64, :, :],
                         src[0].rearrange("(q two) p c -> two p q c", two=2)[h],
                         nm + "h%d" % h, r=[], w=[nm + "h%d" % h])
        bldr = [["bld0h0", "bld0h1"], ["bld1h0", "bld1h1"]]
        S.op("pool", lambda e: e.iota(kv[:, 0:17], [[1, 17]], base=0, channel_multiplier=0,
                                      allow_small_or_imprecise_dtypes=True), w=["kv"])
        S.op("pool", lambda e: e.iota(kv[:, 17:33], [[16, 16]], base=0, channel_multiplier=0,
                                      allow_small_or_imprecise_dtypes=True), w=["kv"])
        self.memset(kv[:, 33:34], 256.0, w=["kv"], eng="pool")
        self.tt(cf[:, 0, :], sc[:, 0, :], sc[:, 2, :], ALU.mult, r=["sc"], w=["cf0"])
        self.tt(cf[:, 1, :], sc[:, 1, :], sc[:, 2, :], ALU.mult, r=["sc"], w=["cf1"])
        kvb = self.bc(kv[:, :], 1, 32)
        self.tt(t[0][:, :, :], self.bc(cf[:, 0, :], 2, NK), kvb, ALU.mult, r=["cf0", "kv"], w=["t0"])
        self.act(t[0][:, :, :], t[0][:, :, :], AF.Exp, r=["t0"], w=["t0"])
        self.tt(t[1][:, :, :], self.bc(cf[:, 1, :], 2, NK), kvb, ALU.mult, r=["cf1", "kv"], w=["t1"])
        self.ts(t[1][:, :, :], t[1][:, :, :], 1.0 / (2 * np.pi), None, ALU.mult, None, r=["t1"], w=["t1"])
        SC = 2 * np.pi * (1.0 - 1e-6)
        for (dst, off, nm) in ((self.tabim, 0.0, "tabim"), (self.tabre, 0.25, "tabre")):
            if off:
                self.ts(t[2][:, :, :], t[1][:, :, :], off, None, ALU.add, None, r=["t1"], w=["t2"])
            else:
                self.cp(t[2][:, :, :], t[1][:, :, :], r=["t1"], w=["t2"])
            self.cp(ti[:, :, :], t[2][:, :, :], r=["t2"], w=["ti"])
            self.cp(t[3][:, :, :], ti[:, :, :], r=["ti"], w=["t3"])
            self.tt(t[2][:, :, :], t[2][:, :, :], t[3][:, :, :], ALU.subtract, r=["t2", "t3"], w=["t2"])
            self.act(t[2][:, :, :], t[2][:, :, :], AF.Sin, r=["t2"], w=["t2"], scale=SC)
            self.tt(dst[:, :, :], t[2][:, :, :], t[0][:, :, :], ALU.mult, r=["t2", "t0"], w=[nm])
        lr, li = sc[:, 0, :], sc[:, 1, :]
        are, aim = self.tabre[:, :, 1], self.tabim[:, :, 1]
        self.ts(cf[:, 2, :], are, -1.0, None, ALU.add, None, r=["tabre"], w=["cf2"])
        self.tt(cf[:, 3, :], lr, lr, ALU.mult, r=["sc"], w=["cf3"])
        self.tt(cf[:, 4, :], li, li, ALU.mult, r=["sc"], w=["cf4"])
        self.tt(cf[:, 3, :], cf[:, 3, :], cf[:, 4, :], ALU.add, r=["cf3", "cf4"], w=["cf3"])
        S.op("dve", lambda e: e.reciprocal(cf[:, 3, :], cf[:, 3, :]), r=["cf3"], w=["cf3"])
        self.tt(cf[:, 4, :], cf[:, 2, :], lr, ALU.mult, r=["cf2", "sc"], w=["cf4"])
        self.tt(cf[:, 5, :], aim, li, ALU.mult, r=["tabim", "sc"], w=["cf5"])
        self.tt(cf[:, 4, :], cf[:, 4, :], cf[:, 5, :], ALU.add, r=["cf4", "cf5"], w=["cf4"])
        self.tt(cf[:, 4, :], cf[:, 4, :], cf[:, 3, :], ALU.mult, r=["cf4", "cf3"], w=["cf4"])
        self.tt(cf[:, 5, :], aim, lr, ALU.mult, r=["tabim", "sc"], w=["cf5"])
        self.tt(cf[:, 6, :], cf[:, 2, :], li, ALU.mult, r=["cf2", "sc"], w=["cf6"])
        self.tt(cf[:, 5, :], cf[:, 5, :], cf[:, 6, :], ALU.subtract, r=["cf5", "cf6"], w=["cf5"])
        self.tt(cf[:, 5, :], cf[:, 5, :], cf[:, 3, :], ALU.mult, r=["cf5", "cf3"], w=["cf5"])
        cre = self.bc(cf[:, 4, :], 2, 16)
        cim = self.bc(cf[:, 5, :], 2, 16)
        self.tt(bp[0][:, :, :], bld[0][:, :, :], cre, ALU.mult, r=bldr[0] + ["cf4"], w=["bp0"])
        self.tt(t[0][:, :, 0:16], bld[1][:, :, :], cim, ALU.mult, r=bldr[1] + ["cf5"], w=["t0"])
        self.tt(bp[0][:, :, :], bp[0][:, :, :], t[0][:, :, 0:16], ALU.subtract, r=["bp0", "t0"], w=["bp0"])
        self.tt(bp[1][:, :, :], bld[1][:, :, :], cre, ALU.mult, r=bldr[1] + ["cf4"], w=["bp1"])
        self.tt(t[1][:, :, 0:16], bld[0][:, :, :], cim, ALU.mult, r=bldr[0] + ["cf5"], w=["t1"])
        self.tt(bp[1][:, :, :], bp[1][:, :, :], t[1][:, :, 0:16], ALU.add, r=["bp1", "t1"], w=["bp1"])
        self.memset(pm[:, :], 0.0, w=["pm"])
        self.memset(pm[0:64, 0:1], 1.0, w=["pm"])
        self.memset(pm[64:128, 1:2], 1.0, w=["pm"])
        for ri in range(2):
            in0 = self.bc(bp[ri][:, :, :], 2, 2)
            in1 = self.bc(self.bc(pm[:, :], 1, 32), 3, 16)
            self.tt(self.bm[ri][:, :, :, :], in0, in1, ALU.mult, r=["bp%d" % ri, "pm"], w=["bm%d" % ri])
            self.cp(self.bmb[ri][:, :, :, :], self.bm[ri][:, :, :, :], r=["bm%d" % ri], w=["bmb%d" % ri])
        S.op("pool", lambda e: e.iota(ipar[:, :], [[0, 1]], base=0, channel_multiplier=1), w=["ipar"])
        S.op("dve", lambda e: e.tensor_scalar(ipar[:, :], ipar[:, :], 4, 1, ALU.arith_shift_right, ALU.bitwise_and),
             r=["ipar"], w=["ipar"])
        self.cp(self.rowpar[:, 1:2], ipar[:, :], r=["ipar"], w=["rowpar"])
        self.ts(self.rowpar[:, 0:1], self.rowpar[:, 1:2], -1.0, 1.0, ALU.mult, ALU.add, r=["rowpar"], w=["rowpar"])
        self.cp(self.dcol[:, :], self.gcol[:, 7, :], r=["gcol"], w=["dcol"])
        self.dbg_out("tabre", self.tabre[:, :, :], [128, 32, NK], r=["tabre"])
        self.dbg_out("tabim", self.tabim[:, :, :], [128, 32, NK], r=["tabim"])
        self.dbg_out("bm0", self.bm[0][:, :, :, :], [128, 32, 2, 16], r=["bm0"])
        self.dbg_out("bm1", self.bm[1][:, :, :, :], [128, 32, 2, 16], r=["bm1"])
        self.barrier()
        self.arena_off = keep

    def s5_pass_a(self):
        S = self.S
        self.X_off = self.arena_off
        self.X = self.carve([128, 32, 2, 128], F32)
        keepX = self.arena_off
        wx = self.carve([128, 2, 16, 128], BF16)
        tmp = [self.carve([128, 8, 4, 32], F32) for _ in range(2)]
        mz = [self.carve([128, 8, 128], BF16) for _ in range(2)]
        psb = [self.ps[5][:, :].bitcast(BF16), self.ps[6][:, :].bitcast(BF16)]
        trr = 0
        for ct in range(NCH):
            p0 = 4 * ct
            for kh in range(2):
                ks = slice(kh * 8, kh * 8 + 8)
                bmr = self.bc(self.bm[0][:, p0:p0 + 4, :, :].rearrange("p g a c -> p g (a c)"), 1, 8)
                bmi = self.bc(self.bm[1][:, p0:p0 + 4, :, :].rearrange("p g a c -> p g (a c)"), 1, 8)
                er = self.bc(self.tabre[:, p0:p0 + 4, ks].rearrange("p g k -> p k g"), 3, 32)
                ei = self.bc(self.tabim[:, p0:p0 + 4, ks].rearrange("p g k -> p k g"), 3, 32)
                mzr = mz[0][:, :, :].rearrange("p k (g c) -> p k g c", g=4)
                mzi = mz[1][:, :, :].rearrange("p k (g c) -> p k g c", g=4)
                self.tt(tmp[0][:, :, :, :], bmr, er, ALU.mult, r=["bm0", "tabre"], w=["tmp0"])
                self.tt(tmp[1][:, :, :, :], bmi, ei, ALU.mult, r=["bm1", "tabim"], w=["tmp1"])
                self.tt(mzr, tmp[0][:, :, :, :], tmp[1][:, :, :, :], ALU.subtract, r=["tmp0", "tmp1"], w=["mz0"])
                self.tt(tmp[0][:, :, :, :], bmr, ei, ALU.mult, r=["bm0", "tabim"], w=["tmp0"])
                self.tt(tmp[1][:, :, :, :], bmi, er, ALU.mult, r=["bm1", "tabre"], w=["tmp1"])
                self.tt(mzi, tmp[0][:, :, :, :], tmp[1][:, :, :, :], ALU.add, r=["tmp0", "tmp1"], w=["mz1"])
                for ri in range(2):
                    for q4 in range(2):
                        pb = trr % 2
                        trr += 1
                        pn = "ps%d" % (5 + pb)
                        for q in range(4):
                            k = q4 * 4 + q
                            self.tr(psb[pb][:, q * 128:(q + 1) * 128], mz[ri][:, k, :], self.identb[:, :],
                                    r=["mz%d" % ri, "identb", "xdone"], w=[pn])
                        k0 = kh * 8 + q4 * 4
                        self.cp(wx[:, ri, k0:k0 + 4, :], psb[pb][:, 0:512].rearrange("p (k n) -> p k n", k=4),
                                r=[pn], w=["wx"], eng="act")
            uv = self.hT[:, ct, :].rearrange("p (n i) -> p n i", i=16)
            for ri in range(2):
                for k in range(16):
                    ip = 15 - k
                    for g in range(4):
                        pr = slice(32 * g, 32 * g + 32)
                        S.op("pe", lambda e, g=g, ri=ri, k=k, ip=ip, pr=pr, uv=uv: e.matmul(
                            self.ps[g][:, ri * 128:(ri + 1) * 128], wx[pr, ri, k, :], uv[pr, :, ip],
                            start=(k == 0), stop=(k == 15), tile_position=(32 * g, 0)),
                            r=["wx", "hT.%d.0" % ct, "hT.%d.1" % ct, "hT.%d.2" % ct, "hT.%d.3" % ct],
                            w=["ps%d" % g])
            for g in range(4):
                self.cp(self.X[:, p0 + g, :, :], self.ps[g][:, 0:256].rearrange("p (r n) -> p r n", r=2),
                        r=["ps%d" % g], w=["X", "xdone"], eng="act")
        self.dbg_out("X", self.X[:, :, :, :], [128, 32, 2, 128], r=["X"])
        self.barrier()
        self.arena_off = keepX
        t = [self.carve([128, 32, 8], F32) for _ in range(4)]
        X5 = self.X[:, :, :, :].rearrange("p t r (b j) -> p t r b j", j=16)
        a16r = self.bc(self.tabre[:, :, 16], 2, 8)
        a16i = self.bc(self.tabim[:, :, 16], 2, 8)
        for j in range(1, 16):
            pr, pi = X5[:, :, 0, :, j - 1], X5[:, :, 1, :, j - 1]
            cr, ci = X5[:, :, 0, :, j], X5[:, :, 1, :, j]
            rp = ["X.%d" % (j - 1)]
            self.tt(t[0][:, :, :], pr, a16r, ALU.mult, r=rp, w=["sc0"])
            self.tt(t[1][:, :, :], pi, a16i, ALU.mult, r=rp, w=["sc1"])
            self.tt(t[2][:, :, :], pr, a16i, ALU.mult, r=rp, w=["sc2"])
            self.tt(t[3][:, :, :], pi, a16r, ALU.mult, r=rp, w=["sc3"])
            self.tt(cr, cr, t[0][:, :, :], ALU.add, r=["sc0"], w=["X.%d" % j])
            self.tt(cr, cr, t[1][:, :, :], ALU.subtract, r=["sc1"], w=["X.%d" % j])
            self.tt(ci, ci, t[2][:, :, :], ALU.add, r=["sc2"], w=["X.%d" % j])
            self.tt(ci, ci, t[3][:, :, :], ALU.add, r=["sc3"], w=["X.%d" % j])
        self.barrier()
        self.arena_off = keepX

    def s5_exchange(self):
        nc = self.nc
        X5 = self.X[:, :, :, :].rearrange("p t r (b j) -> p t r b j", j=16)
        mode = self.cfg.get("s5x", "zero")
        self.Pown = self.carve([128, 32, 2, 8], F32)
        keep = self.arena_off
        if mode == "zero":
            self.memset(self.Pown[:, :, :, :], 0.0, w=["Pown"])
            return
        if mode == "emit":
            f = nc.dram_tensor("f_out", [128, 32, 2, 8], F32, kind="ExternalOutput").ap()
            fs = self.carve([128, 32, 2, 8], F32)
            self.cp(fs[:, :, :, :], X5[:, :, :, :, 15], r=["X.15"], w=["fs"])
            self.dma(f, fs[:, :, :, :], "f_out", r=["fs"], w=["f_out"])
            self.S.op("sp", None, r=["f_out"], w=[])
            return
        sel_d = self.dram_in("blk_sel", [8, 32])
        if mode == "cc":
            f_loc = nc.dram_tensor("f_loc", [128, 512], F32)
            f_gath = nc.dram_tensor("f_gath", [512, 512], F32)
            fs = self.carve([128, 32, 2, 8], F32)
            self.cp(fs[:, :, :, :], X5[:, :, :, :, 15], r=["X.15"], w=["fs"])
            self.dma(f_loc.ap(), fs[:, :, :, :].rearrange("p t r b -> p (t r b)"), "f_loc", r=["fs"], w=["f_loc"])
            self.allgather(f_gath, f_loc, "cc_f", r=["f_loc"], w=["f_gath"])
            fg = self.wbuf[:, 0:4096].bitcast(F32).rearrange("p (q t r b) -> p q t r b", q=4, t=32, r=2)
            self.dma(fg, f_gath.ap().rearrange("(q p) (t r b) -> p q t r b", q=4, t=32, r=2), "fg", r=["f_gath"], w=["fall"])
            inv = {}
            for rr_ in range(4):
                for ii in range(8):
                    inv[zz_block(rr_, ii)] = (rr_, ii)
            fsrc = lambda ri, b: fg[:, inv[b][0], :, ri, inv[b][1]]
        else:
            fall_d = self.dram_in("f_all", [128, 32, 2, 32])
        fall = self.wbuf[:, 0:4096].bitcast(F32).rearrange("p (t r b) -> p t r b", t=32, r=2)
        pall = self.wbuf[:, 4096:8192].bitcast(F32).rearrange("p (t r b) -> p t r b", t=32, r=2)
        tsel = self.wbuf[:, 8192:12288].bitcast(F32).rearrange("p (a b) -> p a b", a=64)
        sel = self.carve([128, 8, 32], F32)
        t = [self.carve([128, 32], F32) for _ in range(4)]
        if mode != "cc":
            self.dma(fall[:, :, :, :], fall_d, "fall", r=[], w=["fall"])
            fsrc = lambda ri, b: fall[:, :, ri, b]
        self.dma(sel[:, :, :], sel_d.partition_broadcast(128), "sel", r=[], w=["sel"])
        self.memset(pall[:, :, :, 0], 0.0, w=["pall.0"])
        Ar, Ai = self.tabre[:, :, 33], self.tabim[:, :, 33]
        for b in range(1, 32):
            pr, pi = pall[:, :, 0, b - 1], pall[:, :, 1, b - 1]
            rp = ["pall.%d" % (b - 1)]
            self.tt(t[0][:, :], pr, Ar, ALU.mult, r=rp, w=["px0"])
            self.tt(t[1][:, :], pi, Ai, ALU.mult, r=rp, w=["px1"])
            self.tt(t[2][:, :], pr, Ai, ALU.mult, r=rp, w=["px2"])
            self.tt(t[3][:, :], pi, Ar, ALU.mult, r=rp, w=["px3"])
            self.tt(t[0][:, :], t[0][:, :], t[1][:, :], ALU.subtract, r=["px0", "px1"], w=["px0"])
            self.tt(pall[:, :, 0, b], t[0][:, :], fsrc(0, b - 1), ALU.add, r=["px0", "fall"], w=["pall.%d" % b])
            self.tt(t[2][:, :], t[2][:, :], t[3][:, :], ALU.add, r=["px2", "px3"], w=["px2"])
            self.tt(pall[:, :, 1, b], t[2][:, :], fsrc(1, b - 1), ALU.add, r=["px2", "fall"], w=["pall.%d" % b])
        pv = pall[:, :, :, :].rearrange("p t r b -> p (t r) b")
        allp = ["pall.%d" % b for b in range(32)]
        for i in range(8):
            self.tt(tsel[:, :, :], pv, self.bc(sel[:, i, :], 1, 64), ALU.mult, r=allp + ["sel"], w=["tsel"])
            self.S.op("dve", lambda e, i=i: e.tensor_reduce(
                self.Pown[:, :, :, i].rearrange("p t r -> p (t r)"), tsel[:, :, :],
                mybir.AxisListType.X, ALU.add), r=["tsel"], w=["Pown"])
        self.barrier()
        self.arena_off = keep

    def s5_pass_b(self):
        S = self.S
        X5 = self.X[:, :, :, :].rearrange("p t r (b j) -> p t r b j", j=16)
        sprev = self.carve([128, 32, 2, 128], BF16)
        sp5 = sprev[:, :, :, :].rearrange("p t r (b j) -> p t r b j", j=16)
        keep = self.arena_off
        tA = self.wout_s[0][:, :, :].rearrange("p a b -> p (a b)").bitcast(F32)[:, 0:512].rearrange("p (a b) -> p a b", a=32)
        tB = self.wout_s[1][:, :, :].rearrange("p a b -> p (a b)").bitcast(F32)[:, 0:512].rearrange("p (a b) -> p a b", a=32)
        Tre, Tim = self.tabre[:, :, 17:33], self.tabim[:, :, 17:33]
        allX = ["X.%d" % j for j in range(16)]
        for b in range(8):
            pre = self.bc(self.Pown[:, :, 0, b], 2, 16)
            pim = self.bc(self.Pown[:, :, 1, b], 2, 16)
            for ri in range(2):
                if ri == 0:
                    self.tt(tA[:, :, :], Tre, pre, ALU.mult, r=["tabre", "Pown"], w=["tA"])
                    self.tt(tB[:, :, :], Tim, pim, ALU.mult, r=["tabim", "Pown"], w=["tB"])
                    self.tt(tA[:, :, :], tA[:, :, :], tB[:, :, :], ALU.subtract, r=["tA", "tB"], w=["tA"])
                else:
                    self.tt(tA[:, :, :], Tre, pim, ALU.mult, r=["tabre", "Pown"], w=["tA"])
                    self.tt(tB[:, :, :], Tim, pre, ALU.mult, r=["tabim", "Pown"], w=["tB"])
                    self.tt(tA[:, :, :], tA[:, :, :], tB[:, :, :], ALU.add, r=["tA", "tB"], w=["tA"])
                self.cp(sp5[:, :, ri, b, 0], tA[:, :, 0], r=["tA"], w=["sprev"])
                self.tt(sp5[:, :, ri, b, 1:16], tA[:, :, 1:16], X5[:, :, ri, b, 0:15], ALU.add,
                        r=["tA"] + allX, w=["sprev"])
        self.dbg_out("sprev", sprev[:, :, :, :], [128, 32, 2, 128], r=["sprev"])
        self.barrier()
        self.arena_off = self.X_off
        cld = [self.carve([128, 64], F32) for _ in range(2)]
        cm = [self.carve([128, 2, 64], F32) for _ in range(2)]
        ctt = [self.carve([128, 128], F32) for _ in range(2)]
        wy = self.carve([128, 4, 2, 17, 32], BF16)
        kw = self.carve([128, 16, 128], BF16)
        tA = self.carve([128, 2, 17, 32], F32)
        tB = self.carve([128, 2, 17, 32], F32)
        ytmp = [self.carve([128, 128, 4], F32) for _ in range(2)]
        self.memset(kw[:, :, :], 0.0, w=["kw"])
        cre_d = self.c_re[0].rearrange("(ct gl) c p -> ct (gl c) p", ct=8)
        cim_d = self.c_im[0].rearrange("(ct gl) c p -> ct (gl c) p", ct=8)
        for ct in range(NCH):
            p0 = 4 * ct
            for ri, src in ((0, cre_d), (1, cim_d)):
                self.dma(cld[ri][:, :], src[ct], "cld%d" % ri, r=[], w=["cld%d" % ri])
                self.tt(cm[ri][:, :, :], self.bc(cld[ri][:, :], 1, 2), self.bc(self.rowpar[:, :], 2, 64), ALU.mult,
                        r=["cld%d" % ri, "rowpar"], w=["cm%d" % ri])
                self.tr(self.ps[5][:, ri * 128:(ri + 1) * 128], cm[ri][:, :, :].rearrange("p a b -> p (a b)"),
                        self.ident[:, :], r=["cm%d" % ri, "ident"], w=["ps5"])
                self.cp(ctt[ri][:, :], self.ps[5][:, ri * 128:(ri + 1) * 128], r=["ps5"], w=["ctt%d" % ri], eng="act")
            for gh in range(2):
                gsl = slice(2 * gh, 2 * gh + 2)
                cr = self.bc(ctt[0][:, :].rearrange("p (g c) -> p g c", g=4)[:, gsl, :], 2, 17)
                ci = self.bc(ctt[1][:, :].rearrange("p (g c) -> p g c", g=4)[:, gsl, :], 2, 17)
                er = self.bc(self.tabre[:, p0 + 2 * gh:p0 + 2 * gh + 2, 0:17], 3, 32)
                ei = self.bc(self.tabim[:, p0 + 2 * gh:p0 + 2 * gh + 2, 0:17], 3, 32)
                self.tt(tA[:, :, :, :], cr, er, ALU.mult, r=["ctt0", "tabre"], w=["wtA"])
                self.tt(tB[:, :, :, :], ci, ei, ALU.mult, r=["ctt1", "tabim"], w=["wtB"])
                self.tt(wy[:, gsl, 0, :, :], tA[:, :, :, :], tB[:, :, :, :], ALU.subtract, r=["wtA", "wtB"], w=["wy0"])
                self.tt(tA[:, :, :, :], cr, ei, ALU.mult, r=["ctt0", "tabim"], w=["wtA"])
                self.tt(tB[:, :, :, :], ci, er, ALU.mult, r=["ctt1", "tabre"], w=["wtB"])
                self.stt(wy[:, gsl, 1, :, :], tA[:, :, :, :], -1.0, tB[:, :, :, :], ALU.mult, ALU.subtract,
                         r=["wtA", "wtB"], w=["wy1"])
            for g in range(4):
                pr = slice(32 * g, 32 * g + 32)
                for ri in range(2):
                    S.op("pe", lambda e, g=g, ri=ri, pr=pr, p0=p0: e.matmul(
                        self.ps[4][pr, :], self.bmb[ri][:, p0 + g, :, :].rearrange("p a c -> p (a c)"),
                        wy[:, g, ri, 0:16, :], start=(ri == 0), stop=(ri == 1), tile_position=(0, 32 * g)),
                        r=["bmb%d" % ri, "wy%d" % ri], w=["ps4"])
            for g in range(4):
                pr = slice(32 * g, 32 * g + 32)
                self.cp(kw[pr, :, 32 * g:32 * g + 32], self.ps[4][pr, :].rearrange("p (t c) -> p t c", t=16),
                        r=["ps4"], w=["kw"], eng="act")
            uv = self.hT[:, ct, :].rearrange("p (n i) -> p n i", i=16)
            hres = ["hT.%d.%d" % (ct, t) for t in range(NTB)]
            for b in range(4):
                pv = self.ps[b][:, :].rearrange("p (n q) -> p n q", q=4)
                for tau in range(4 * b + 4):
                    q0 = max(0, tau - 4 * b)
                    self.mm(pv[:, :, q0:4], kw[:, tau, :], uv[:, :, 4 * b + q0 - tau:4 * b + 4 - tau],
                            tau == 0, False, r=["kw"] + hres, w=["ps%d" % b])
                for q in range(4):
                    i = 4 * b + q
                    for g in range(4):
                        pr = slice(32 * g, 32 * g + 32)
                        for ri in range(2):
                            last = (q == 3 and g == 3 and ri == 1)
                            S.op("pe", lambda e, g=g, ri=ri, pr=pr, i=i, q=q, b=b, last=last, p0=p0: e.matmul(
                                self.ps[b][pr, :].rearrange("p (n q) -> p n q", q=4)[:, :, q],
                                wy[:, g, ri, i + 1, :], sprev[:, p0 + g, ri, :],
                                start=False, stop=last, tile_position=(0, 32 * g)),
                                r=["wy%d" % ri, "sprev"], w=["ps%d" % b])
            for b in range(4):
                pv = self.ps[b][:, :].rearrange("p (n q) -> p n q", q=4)
                yt = ytmp[b % 2]
                self.stt(yt[:, :, :], uv[:, :, 4 * b:4 * b + 4], self.dcol[:, ct:ct + 1], pv, ALU.mult, ALU.add,
                         r=["ps%d" % b, "dcol"] + hres, w=["ytmp%d" % (b % 2)])
                self.act(uv[:, :, 4 * b:4 * b + 4], yt[:, :, :], AF.Gelu_apprx_tanh,
                         r=["ytmp%d" % (b % 2)], w=hres)

    def glu(self):
        wg = self.wglu[0].rearrange("(k p) f -> p k f", p=128)
        sg = [self.carve([128, TB], F32) for _ in range(2)]
        for gs in range(4):
            s = self.win_rr % 3
            self.win_rr += 1
            wsl = self.win_s[s]
            wn = "win%d" % s
            c0 = gs * 256
            self.dma(wsl[:, :, 0, :], wg[:, :, c0:c0 + 256], wn + "g", r=[], w=[wn + "g"], eng="pool")
            self.dma(wsl[:, :, 1, :], wg[:, :, D + c0:D + c0 + 256], wn + "u", r=[], w=[wn + "u"], eng="pool")
            for jl in range(2):
                m = 2 * gs + jl
                for t in range(NTB):
                    tb = slice(t * TB, (t + 1) * TB)
                    pa = 2 * (self.mm1_rr % 2)
                    self.mm1_rr += 1
                    for gu in range(2):
                        for k in range(NCH):
                            self.mm(self.ps[pa + gu][:, :], wsl[:, k, gu, jl * 128:(jl + 1) * 128],
                                    self.hT[:, k, tb], k == 0, k == NCH - 1,
                                    r=[wn + "gu"[gu], "hT.%d.%d" % (k, t)], w=["ps%d" % (pa + gu)])
                    sgt = sg[pa // 2]
                    sgn = "gsg%d" % (pa // 2)
                    self.act(sgt[:, :], self.ps[pa + 1][:, :], AF.Sigmoid, r=["ps%d" % (pa + 1)], w=[sgn])
                    self.tt(sgt[:, :], sgt[:, :], self.ps[pa][:, :], ALU.mult, r=[sgn, "ps%d" % pa], w=[sgn])
                    self.tt(self.xT[:, m, tb], self.xT[:, m, tb], sgt[:, :], ALU.add,
                            r=[sgn, "xT.%d.%d" % (m, t)], w=["xT.%d.%d" % (m, t)])

    def s5_mixer(self, n):
        self.phase("s5")
        self.hT = self.carve([128, NCH, NT], BF16)
        self.sq = self.carve([128, NCH, TB], BF16)
        self.rstd = [self.carve([128, TB], F32) for _ in range(2)]
        off_after_h = NCH * NT * 2
        self.rmsnorm(n)
        self.barrier()
        self.arena_off = off_after_h
        self.s5_tables()
        self.s5_pass_a()
        self.s5_exchange()
        if self.cfg.get("s5x") == "emit":
            return False
        self.s5_pass_b()
        self.barrier()
        self.arena_off = off_after_h
        self.glu()
        return True


    def rope_consts(self):
        S = self.S
        rowc = self.carve([1, 64], F32)
        self.ropec = self.carve([32, 2], F32)
        for d in range(32):
            j = d % 16
            self.memset(rowc[0:1, d:d + 1], float(np.float32(500000.0) ** np.float32(-(2.0 * j) / 32.0)), w=["rowc"])
        self.memset(rowc[0:1, 32:48], -1.0, w=["rowc"])
        self.memset(rowc[0:1, 48:64], 1.0, w=["rowc"])
        for i in range(2):
            self.mm(self.ps[7][0:32, 2 * i:2 * i + 2], rowc[0:1, 32 * i:32 * i + 32], self.ident[0:1, 0:2].bitcast(F32)
                    if False else rowc[0:1, 48:50], True, True, r=["rowc"], w=["ps7"])
        self.cp(self.ropec[:, :], self.ps[7][0:32, 0:4:2], r=["ps7"], w=["ropec"])
        self.pos_d = self.din["pos"].ap() if "pos" in self.din else self.dram_in("pos", [1, NT])

    def rope_tables(self, t0, n, bufs):
        posb, cosb, sinb, t1, ti = bufs
        rs = self.cfg.get("kv_skip", ())
        if "rope" in rs:
            self.memset(cosb[:, :], 1.0, w=["cosb"])
            self.memset(sinb[:, :], 0.0, w=["sinb"])
            return cosb, sinb
        self.dma(posb[:, :], self.pos_d[0:1, t0:t0 + n].partition_broadcast(32) if False else
                 self.pos_d[0, t0:t0 + n].partition_broadcast(32), "posb", r=[], w=["posb"])
        invf = self.ropec[:, 0:1]
        sgn = self.ropec[:, 1:2]
        self.ts(t1[:, :], posb[:, :], invf, 1.0 / (2 * np.pi), ALU.mult, ALU.mult, r=["posb", "ropec"], w=["rt1"])
        SC = 2 * np.pi * (1.0 - 1e-6)
        for (dst, off, nm) in ((sinb, 0.0, "sinb"), (cosb, 0.25, "cosb")):
            if off:
                self.ts(dst[:, :], t1[:, :], off, None, ALU.add, None, r=["rt1"], w=[nm])
            else:
                self.cp(dst[:, :], t1[:, :], r=["rt1"], w=[nm])
            self.cp(ti[:, :], dst[:, :], r=[nm], w=["rti"])
            self.cp(posb[:, :], ti[:, :], r=["rti"], w=["posb"])
            self.tt(dst[:, :], dst[:, :], posb[:, :], ALU.subtract, r=[nm, "posb"], w=[nm])
            self.act(dst[:, :], dst[:, :], AF.Sin, r=[nm], w=[nm], scale=SC)
        self.ts(sinb[:, :], sinb[:, :], sgn, None, ALU.mult, None, r=["sinb", "ropec"], w=["sinb"])
        return cosb, sinb

    def kv_phase(self):
        nc, S = self.nc, self.S
        self.phase("kv")
        self.hT = self.carve([128, NCH, NT], BF16)
        self.sq = self.carve([128, NCH, TB], BF16)
        self.rstd = [self.carve([128, TB], F32) for _ in range(2)]
        self.rmsnorm(8)
        emit = self.cfg.get("kvx") == "emit"
        kind = "ExternalOutput" if emit else "Internal"
        self.kT_loc_t = nc.dram_tensor("kT_loc", [128, 2 * NT], BF16, kind=kind)
        self.v_loc_t = nc.dram_tensor("v_loc", [128, 16 * 2 * 128], BF16, kind=kind)
        self.km_loc_t = nc.dram_tensor("km_loc", [128, 16], F32, kind=kind)
        self.kT_loc_d = self.kT_loc_t.ap().rearrange("p (h t) -> p h t", h=2)
        self.v_loc_d = self.v_loc_t.ap().rearrange("p (a d) -> p a d", a=32)
        self.km_loc_d = self.km_loc_t.ap().rearrange("p (h b) -> p h b", h=2)
        wk_d = self.dram_in("w_k", [D, 256]).rearrange("(k p) n -> p k n", p=128)
        wv_d = self.dram_in("w_v", [D, 256]).rearrange("(k p) n -> p k n", p=128)
        wk = self.wbuf[:, 0:2048].rearrange("p (k n) -> p k n", k=NCH)
        wv = self.wbuf[:, 2048:4096].rearrange("p (k n) -> p k n", k=NCH)
        wks = self.wbuf[:, 4096:4608].rearrange("p (k h n) -> p k h n", k=NCH, h=2)
        self.dma(wk, wk_d, "wk", r=[], w=["wk"], eng="pool")
        self.dma(wv, wv_d, "wv", r=[], w=["wv"], eng="pool")
        if "wks" in self.cfg.get("kv_skip", ()):
            self.memset(wks[:, :, :, :], 0.0, w=["wks0a", "wks0b", "wks1a", "wks1b"])
        for h in range(2 if "wks" not in self.cfg.get("kv_skip", ()) else 0):
            self.dma(wks[:, :, h, 0:16], wk_d[:, :, h * 128 + 16:h * 128 + 32], "wks%da" % h, r=[], w=["wks%da" % h], eng="pool")
            self.dma(wks[:, :, h, 16:32], wk_d[:, :, h * 128:h * 128 + 16], "wks%db" % h, r=[], w=["wks%db" % h], eng="pool")
        kT = self.carve([128, 2, NT], BF16)
        vl = self.carve([128, 16, 2, 130], BF16)
        km = self.carve([128, 2, 8], F32)
        self.rope_consts()
        rb = [self.carve([32, TB], F32) for _ in range(4)] + [self.carve([32, TB], I32)]
        tq = [self.carve([32, TB], F32) for _ in range(2)]
        self.memset(vl[:, :, :, 128:130], 1.0, w=["vl1"])
        cnt = 0
        kvs = self.cfg.get("kv_skip", ())
        if "k" in kvs:
            self.memset(kT[:, :, :], 0.0, w=["kT.%d.%d" % (h, t) for h in range(2) for t in range(NTB)])
        for t in range(NTB if "k" not in kvs else 0):
            tb = slice(t * TB, (t + 1) * TB)
            cosb, sinb = self.rope_tables(t * TB, TB, rb)
            for h in range(2):
                pk, psw = cnt % 2, 2 + cnt % 2
                cnt += 1
                for k in range(NCH):
                    self.mm(self.ps[pk][:, :], wk[:, k, h * 128:(h + 1) * 128], self.hT[:, k, tb], k == 0, k == NCH - 1,
                            r=["wk", "hT.%d.%d" % (k, t)], w=["ps%d" % pk])
                for k in range(NCH):
                    self.mm(self.ps[psw][0:32, :], wks[:, k, h, :], self.hT[:, k, tb], k == 0, k == NCH - 1,
                            r=["wks%da" % h, "wks%db" % h, "hT.%d.%d" % (k, t)], w=["ps%d" % psw])
                self.cp(kT[:, h, tb], self.ps[pk][:, :], r=["ps%d" % pk], w=["kT.%d.%d" % (h, t)], eng="act")
                if "krope" in kvs:
                    continue
                self.tt(tq[0][:, :], self.ps[pk][0:32, :], cosb[:, :], ALU.mult, r=["ps%d" % pk, "cosb"], w=["tq0"])
                self.tt(tq[1][:, :], self.ps[psw][0:32, :], sinb[:, :], ALU.mult, r=["ps%d" % psw, "sinb"], w=["tq1"])
                self.tt(kT[0:32, h, tb], tq[0][:, :], tq[1][:, :], ALU.add, r=["tq0", "tq1"], w=["kT.%d.%d" % (h, t)])
        if "v" in kvs:
            self.memset(vl[:, :, :, 0:128], 0.0, w=["vl.%d" % i for i in range(16)])
        for tt_ in range(16 if "v" not in kvs else 0):
            pv = 4 + tt_ % 2
            for k in range(NCH):
                self.mm(self.ps[pv][:, 0:256], self.hT[:, k, tt_ * 128:(tt_ + 1) * 128], wv[:, k, :], k == 0, k == NCH - 1,
                        r=["wv", "hT.%d.%d" % (k, tt_ // 4)], w=["ps%d" % pv])
            self.cp(vl[:, tt_, :, 0:128], self.ps[pv][:, 0:256].rearrange("p (h d) -> p h d", h=2),
                    r=["ps%d" % pv], w=["vl.%d" % tt_], eng=("act" if tt_ % 2 else "dve"))
        allk = ["kT.%d.%d" % (h, t) for h in range(2) for t in range(NTB)]
        if "km" in kvs:
            self.memset(km[:, :, :], 0.0, w=["km"])
        for h in range(2 if "km" not in kvs else 0):
            S.op("dve", lambda e, h=h: e.tensor_reduce(km[:, h, :], kT[:, h, :].rearrange("p (b n) -> p b n", b=8),
                                                      mybir.AxisListType.X, ALU.add), r=allk, w=["km"])
        self.ts(km[:, :, :], km[:, :, :], 1.0 / 256.0, None, ALU.mult, None, r=["km"], w=["km"])
        self.dma(self.kT_loc_d, kT[:, :, :], "kTd", r=allk, w=["kTd"])
        self.dma(self.v_loc_d, vl[:, :, :, 0:128].rearrange("p a h d -> p (a h) d"), "vld",
                 r=["vl1"] + ["vl.%d" % i for i in range(16)], w=["vld"])
        self.dma(self.km_loc_d, km[:, :, :], "kmd", r=["km"], w=["kmd"])
        S.op("sp", None, r=["kTd", "vld", "kmd"], w=[])
        if self.cfg.get("kvx") == "cc":
            self.kT_g = nc.dram_tensor("kT_gath", [512, 2 * NT], BF16)
            self.v_g = nc.dram_tensor("v_gath", [512, 16 * 2 * 128], BF16)
            self.km_g = nc.dram_tensor("km_gath", [512, 16], F32)
            self.allgather(self.kT_g, self.kT_loc_t, "cc_k", r=["kTd"], w=["kTg"])
            self.allgather(self.v_g, self.v_loc_t, "cc_v", r=["vld"], w=["vg"])
            self.allgather(self.km_g, self.km_loc_t, "cc_m", r=["kmd"], w=["kmg"])

    def attn_phase(self, n):
        nc, S = self.nc, self.S
        self.phase("attn")
        QB = 256
        cc = self.cfg.get("kvx") == "cc"
        if not cc:
            ktall_d = self.dram_in("kT_all", [128, 2, 32, 256], BF16)
            vall_d = self.dram_in("v_all", [128, 64, 2, 130], BF16)
            kmall_d = self.dram_in("km_all", [128, 2, 32])
        vbias_d = self.dram_in("vbias", [8, 32])
        wq_d = self.dram_in("w_q", [1, D, D])[0].rearrange("(k p) n -> p k n", p=128)
        wo_d = self.dram_in("w_o", [1, D, D])[0].rearrange("(h p) n -> p h n", p=128)
        wq = self.wbuf[:, 0:8192].rearrange("p (k n) -> p k n", k=NCH)
        wo = self.wbuf[:, 8192:16384].rearrange("p (h n) -> p h n", h=8)
        wqs = self.wbuf[:, 16384:18432].rearrange("p (k h n) -> p k h n", k=NCH, h=8)
        self.dma(wq, wq_d, "wq", r=[], w=["wq"], eng="pool")
        self.dma(wo, wo_d, "wo", r=[], w=["wo"], eng="pool")
        wqv = wq_d.rearrange("p k (h n) -> p k h n", h=8)
        for k in range(NCH):
            self.dma(wqs[:, k, :, 0:16], wqv[:, k, :, 16:32], "wqs", r=[], w=["wqsa%d" % k], eng="pool")
            self.dma(wqs[:, k, :, 16:32], wqv[:, k, :, 0:16], "wqs", r=[], w=["wqsb%d" % k], eng="pool")
        KT = self.carve([128, 2, 32, 256], BF16)
        VA = self.carve([128, 64, 2, 130], BF16)
        kmf = self.carve([128, 2, 32], F32)
        kmb = self.carve([128, 2, 32], BF16)
        vb = self.carve([128, 8, 32], F32)
        self.KTres, self.VAres = ["KT"], ["VA"]
        if not cc:
            self.dma(KT[:, :, :, :], ktall_d, "KT", r=[], w=["KT"])
            self.dma(VA[:, :, :, :], vall_d, "VA", r=[], w=["VA"])
            self.dma(kmf[:, :, :], kmall_d, "kmf", r=[], w=["kmf"])
        else:
            kg_ = self.kT_g.ap().rearrange("(q p) (h i n) -> q p h i n", q=4, h=2, i=8)
            vg_ = self.v_g.ap().rearrange("(q p) (i a d) -> q p i a d", q=4, i=8, a=4)
            mg_ = self.km_g.ap().rearrange("(q p) (h i) -> q p h i", q=4, h=2)
            VAv = VA[:, :, :, :].rearrange("p (m a) h d -> p m (a h d)", m=4)
            self.top8_ = self.carve([128, 8, 8], F32)
            kmg = self.top8_[:, :, :].rearrange("p a b -> p (a b)").rearrange("p (q h i) -> p q h i", q=4, h=2)
            KTr, VAr = [], []
            for q in range(4):
                for par in range(2):
                    g0 = q if par == 0 else 7 - q
                    for h in range(2):
                        nm = "KT.%d.%d.%d" % (q, par, h)
                        self.dma(KT[:, h, g0::8, :], kg_[q][:, h, par::2, :], "KTall", r=["kTg"], w=[nm])
                        KTr.append(nm)
                    for m in range(4):
                        nm = "VA.%d.%d.%d" % (q, par, m)
                        gblk = 8 * m + g0
                        self.dma(VA[:, 2 * gblk:2 * gblk + 2, :, 0:128].rearrange("p a h d -> p (a h) d"),
                                 vg_[q][:, 2 * m + par, :, :], "VAall", r=["vg", "vones"], w=[nm])
                        VAr.append(nm)
                self.dma(kmg[:, q, :, :], mg_[q], "kmg%d" % q, r=["kmg"], w=["kmgs%d" % q])
                self.cp(kmf[:, :, q::8], kmg[:, q, :, 0::2], r=["kmgs%d" % q], w=["kmf"])
                self.cp(kmf[:, :, 7 - q::8], kmg[:, q, :, 1::2], r=["kmgs%d" % q], w=["kmf"])
            self.KTres, self.VAres = KTr, VAr
        self.cp(kmb[:, :, :], kmf[:, :, :], r=["kmf"], w=["kmb"])
        self.dma(vb[:, :, :], vbias_d.partition_broadcast(128), "vb", r=[], w=["vb"])
        self.rope_consts()
        self.memset(VA[:, :, :, 128:130], 1.0, w=["vones"])
        hq = self.carve([128, NCH, QB], BF16)
        qT = self.carve([128, 8, QB], BF16)
        sqb = self.carve([128, NCH, QB], BF16)
        rsb = self.carve([128, QB], F32)
        rb = [self.carve([32, QB], F32) for _ in range(4)] + [self.carve([32, QB], I32)]
        tq = [self.carve([32, QB], F32) for _ in range(2)]
        kown = [self.carve([128, 2, 256], BF16) for _ in range(2)]
        vown = [self.carve([128, 2, 2, 130], BF16) for _ in range(2)]
        pT = [self.carve([128, 2, 512], BF16) for _ in range(2)]
        acc = self.carve([128, 8, 130], F32)
        gsb = self.carve([128, 8, 32], F32)
        sel = self.carve([128, 8, 32], F32)
        top8 = self.top8_ if cc else self.carve([128, 8, 8], F32)
        thr = self.carve([128, 8], F32)
        rinv = self.carve([128, 8], F32)
        otm = self.carve([128, 8, 128], BF16)
        oT = self.carve([128, 8, 128], BF16)
        tri = self.carve([128, 128], BF16)
        for vv_ in vown:
            self.memset(vv_[:, :, :, 128:130], 1.0, w=["vones"])
        self.ts(tri[:, :], self.iot[:, :], 0.0, None, ALU.is_ge, None, r=["iot"], w=["tri"])
        g = self.gcol
        QSC = float(128.0 ** -0.5)
        psb7 = self.ps[7][:, :].bitcast(BF16)
        npm_tab = [3, 7, 11, 15, 19, 23, 27, 31]
        for lb in range(8):
            t0 = lb * QB
            ts_ = slice(t0, t0 + QB)
            tix = t0 // TB
            ko, vo = kown[lb % 2], vown[lb % 2]
            kon, von = "kown%d" % (lb % 2), "vown%d" % (lb % 2)
            self.dma(ko[:, :, :], self.kT_loc_d[:, :, ts_], kon, r=["kTd"], w=[kon])
            self.dma(vo[:, :, :, 0:128].rearrange("p a h d -> p (a h) d"), self.v_loc_d[:, 4 * lb:4 * lb + 4, :],
                     von, r=["vld", "vones"], w=[von])
            for c in range(NCH):
                self.act(sqb[:, c, :], self.xT[:, c, ts_], AF.Square, r=["xT.%d.%d" % (c, tix)], w=["sqb.%d" % c])
            for c in range(NCH):
                self.mm(self.ps[6][:, 0:QB], self.onesm[:, :], sqb[:, c, :], c == 0, c == NCH - 1,
                        r=["onesm", "sqb.%d" % c], w=["ps6"])
            self.ts(rsb[:, :], self.ps[6][:, 0:QB], EPS, None, ALU.add, None, r=["ps6"], w=["rsb"])
            self.act(rsb[:, :], rsb[:, :], AF.Sqrt, r=["rsb"], w=["rsb"])
            S.op("dve", lambda e: e.reciprocal(rsb[:, :], rsb[:, :]), r=["rsb"], w=["rsb"])
            for c in range(NCH):
                self.stt(hq[:, c, :], self.xT[:, c, ts_], g[:, n, c:c + 1], rsb[:, :], ALU.mult, ALU.mult,
                         r=["xT.%d.%d" % (c, tix), "gcol", "rsb"], w=["hq.%d" % c])
            hqr = ["hq.%d" % c for c in range(NCH)]
            cosb, sinb = self.rope_tables(t0, QB, rb)
            for h in range(8):
                pq = h % 2
                for k in range(NCH):
                    self.mm(self.ps[pq][:, 0:QB], wq[:, k, h * 128:(h + 1) * 128], hq[:, k, :], k == 0, k == NCH - 1,
                            r=["wq", "hq.%d" % k], w=["ps%d" % pq])
                for k in range(NCH):
                    self.mm(self.ps[2 + pq][0:32, 0:QB], wqs[:, k, h, :], hq[:, k, :], k == 0, k == NCH - 1,
                            r=["wqsa%d" % kk for kk in range(NCH)] + ["wqsb%d" % kk for kk in range(NCH)] + ["hq.%d" % k],
                            w=["ps%d" % (2 + pq)])
                self.act(qT[:, h, :], self.ps[pq][:, 0:QB], AF.Copy, r=["ps%d" % pq], w=["qT.%d" % h], scale=QSC)
                self.stt(tq[0][:, :], self.ps[pq][0:32, 0:QB], QSC, cosb[:, :], ALU.mult, ALU.mult,
                         r=["ps%d" % pq, "cosb"], w=["tq0"])
                self.stt(tq[1][:, :], self.ps[2 + pq][0:32, 0:QB], QSC, sinb[:, :], ALU.mult, ALU.mult,
                         r=["ps%d" % (2 + pq), "sinb"], w=["tq1"])
                self.tt(qT[0:32, h, :], tq[0][:, :], tq[1][:, :], ALU.add, r=["tq0", "tq1"], w=["qT.%d" % h])
            qres = ["qT.%d" % h for h in range(8)]
            npm = npm_tab[lb]
            for hf in range(2):
                qs = slice(hf * 128, hf * 128 + 128)
                xs = slice(t0 + hf * 128, t0 + hf * 128 + 128)
                for h in range(8):
                    self.mm(self.ps[6][:, h * 32:(h + 1) * 32], qT[:, h, qs], kmb[:, h // 4, :], True, True,
                            r=["qT.%d" % h, "kmb"], w=["ps6"])
                self.tt(gsb[:, :, :], self.ps[6][:, 0:256].rearrange("p (h n) -> p h n", h=8),
                        self.bc(vb[:, lb, :], 1, 8), ALU.add, r=["ps6", "vb"], w=["gsb"])
                for h in range(8):
                    S.op("dve", lambda e, h=h: e.max(top8[:, h, :], gsb[:, h, :]), r=["gsb"], w=["top8"])
                self.ts(thr[:, :], top8[:, :, 2], -1e29, None, ALU.max, None, r=["top8"], w=["thr"])
                self.tt(sel[:, :, :], gsb[:, :, :], self.bc(thr[:, :], 2, 32), ALU.is_ge, r=["gsb", "thr"], w=["sel"])
                for kg in range(2):
                    hs = slice(4 * kg, 4 * kg + 4)
                    first = True
                    for nb in [-1] + list(range(npm)):
                        pi = self.attn_rr % 2
                        self.attn_rr += 1
                        pS = (0 + 2 * pi, 1 + 2 * pi)
                        pO = (4 + 2 * pi, 5 + 2 * pi)
                        pt = pT[pi]
                        ptn = "pT%d" % pi
                        halves = [0, 1]
                        if nb < 0 and hf == 0:
                            halves = [0]
                        for kh in halves:
                            if nb < 0:
                                lhs = ko[:, kg, kh * 128:(kh + 1) * 128]
                                rr = [kon]
                            else:
                                lhs = KT[:, kg, nb, kh * 128:(kh + 1) * 128]
                                rr = self.KTres
                            self.mm(self.ps[pS[kh]][:, :], lhs, qT[:, hs, qs], True, True,
                                    r=rr + qres[4 * kg:4 * kg + 4], w=["ps%d" % pS[kh]])
                            self.act(pt[:, kh, :], self.ps[pS[kh]][:, :], AF.Exp, r=["ps%d" % pS[kh]], w=[ptn + ".%d" % kh])
                        if nb < 0:
                            khd = hf
                            self.tt(pt[:, khd, :].rearrange("p (h t) -> p h t", h=4),
                                    pt[:, khd, :].rearrange("p (h t) -> p h t", h=4),
                                    self.bc(tri[:, :], 1, 4), ALU.mult, r=[ptn + ".%d" % khd, "tri"], w=[ptn + ".%d" % khd])
                        for hh in range(4):
                            po = pO[hh // 2]
                            oview = self.ps[po][:, (hh % 2) * 130:(hh % 2) * 130 + 130]
                            for kh in halves:
                                if nb < 0:
                                    rhs = vo[:, kh, kg, :]
                                    rr = [von]
                                else:
                                    rhs = VA[:, 2 * nb + kh, kg, :]
                                    rr = self.VAres
                                self.mm(oview, pt[:, kh, hh * 128:(hh + 1) * 128], rhs, kh == halves[0], kh == halves[-1],
                                        r=rr + [ptn + ".%d" % kh], w=["ps%d" % po])
                        for hh in range(4):
                            h = 4 * kg + hh
                            po = pO[hh // 2]
                            oview = self.ps[po][:, (hh % 2) * 130:(hh % 2) * 130 + 130]
                            if nb < 0:
                                self.cp(acc[:, h, :], oview, r=["ps%d" % po], w=["acc.%d" % h])
                            else:
                                self.stt(acc[:, h, :], oview, sel[:, h, nb:nb + 1], acc[:, h, :], ALU.mult, ALU.add,
                                         r=["ps%d" % po, "sel", "acc.%d" % h], w=["acc.%d" % h])
                accr = ["acc.%d" % h for h in range(8)]
                S.op("dve", lambda e: e.reciprocal(rinv[:, :], acc[:, :, 128]), r=accr, w=["rinv"])
                for h in range(8):
                    self.ts(otm[:, h, :], acc[:, h, 0:128], rinv[:, h:h + 1], None, ALU.mult, None,
                            r=["acc.%d" % h, "rinv"], w=["otm"])
                for h in range(8):
                    self.tr(psb7[:, h * 128:(h + 1) * 128], otm[:, h, :], self.identb2[:, :], r=["otm", "identb2"], w=["ps7"])
                self.cp(oT[:, :, :], psb7[:, 0:1024].rearrange("p (h t) -> p h t", h=8), r=["ps7"], w=["oT"], eng="act")
                for cg in range(2):
                    pw = self.ps[6] if cg == 0 else self.ps[7]
                    pwn = "ps6" if cg == 0 else "ps7"
                    for cc in range(4):
                        c = 4 * cg + cc
                        for h in range(8):
                            self.mm(pw[:, cc * 128:(cc + 1) * 128], wo[:, h, c * 128:(c + 1) * 128], oT[:, h, :],
                                    h == 0, h == 7, r=["wo", "oT"], w=[pwn])
                    self.tt(self.xT[:, 4 * cg:4 * cg + 4, xs], self.xT[:, 4 * cg:4 * cg + 4, xs],
                            pw[:, :].rearrange("p (c t) -> p c t", c=4), ALU.add,
                            r=[pwn] + ["xT.%d.%d" % (4 * cg + cc, tix) for cc in range(4)],
                            w=["xT.%d.%d" % (4 * cg + cc, tix) for cc in range(4)])


def build(cfg):
    b = Builder(cfg)
    b.setup_common()
    b.load_x()
    stages = cfg.get("stages", 99)
    cont = True
    skip = cfg.get("skip", ())
    if stages >= 1 and 1 not in skip:
        b.ffn(0, 0, 0)
    if stages >= 2 and 2 not in skip:
        cont = b.s5_mixer(1)
    if cont and stages >= 3 and 3 not in skip:
        b.ffn(0, 1, 2)
    if cont and stages >= 4:
        b.kv_phase()
        if cfg.get("kvx") == "emit":
            cont = False
    if cont and stages >= 5:
        b.ffn(1, 0, 3)
    if cont and stages >= 6:
        b.attn_phase(4)
    if cont and stages >= 7:
        b.ffn(1, 1, 5)
    if cont or cfg.get("store_anyway"):
        b.store_x(final_norm=cfg.get("final_norm", False))
    if b.dbg_keys:
        b.S.op("sp", None, r=b.dbg_keys, w=[])
    b.S.emit()
    return b


def shard_tokens(x):
    out = []
    for core in range(N_CORES):
        bb, r = core // 4, core % 4
        blocks = [zz_block(r, i) for i in range(8)]
        out.append(np.ascontiguousarray(
            np.concatenate([x[bb, g * 256:(g + 1) * 256, :] for g in blocks], axis=0)))
    return out


def unshard_tokens(parts):
    full = np.zeros((2, 8192, D), dtype=parts[0].dtype)
    for core in range(N_CORES):
        bb, r = core // 4, core % 4
        for i in range(8):
            g = zz_block(r, i)
            full[bb, g * 256:(g + 1) * 256, :] = parts[core][i * 256:(i + 1) * 256, :]
    return full


def core_meta(core):
    bb, r = core // 4, core % 4
    blocks = [zz_block(r, i) for i in range(8)]
    pos = np.concatenate([np.arange(g * 256, (g + 1) * 256) for g in blocks]).astype(np.float32)[None, :]
    sel = np.zeros((8, 32), np.float32)
    vbias = np.zeros((8, 32), np.float32)
    for i, g in enumerate(blocks):
        sel[i, g] = 1.0
        vbias[i, g:] = -1e30
    return {"pos": pos, "blk_sel": sel, "vbias": vbias}


def run(cfg, inputs, extra=None, trace=False):
    b = build(cfg)
    xs = shard_tokens(np.asarray(inputs["x"], dtype=np.float32))
    in_maps = []
    for core in range(N_CORES):
        m = {}
        meta = core_meta(core)
        for name in b.din:
            if name == "x":
                m[name] = xs[core]
            elif name in meta:
                m[name] = meta[name]
            elif extra is not None and name in extra[core]:
                m[name] = extra[core][name]
            else:
                m[name] = np.ascontiguousarray(np.asarray(inputs[name], dtype=np.float32))
        in_maps.append(m)
    res = run_bass_kernel_spmd(b.nc, in_maps, core_ids=list(range(N_CORES)), trace=trace)
    return res


def relay_f(results):
    out = []
    for bb in range(2):
        fa = np.zeros((128, 32, 2, 32), np.float32)
        for r in range(4):
            f = np.asarray(results[4 * bb + r]["f_out"])
            for i in range(8):
                fa[:, :, :, zz_block(r, i)] = f[:, :, :, i]
        out.append(fa)
    return [{"f_all": out[c // 4]} for c in range(N_CORES)]


def relay_kv(results, extra):
    for bb in range(2):
        r0 = results[4 * bb]
        kt = np.zeros((128, 2, 32, 256), np.asarray(r0["kT_loc"]).dtype)
        va = np.zeros((128, 64, 2, 130), np.asarray(r0["v_loc"]).dtype)
        km = np.zeros((128, 2, 32), np.float32)
        for r in range(4):
            res = results[4 * bb + r]
            k = np.asarray(res["kT_loc"]).reshape(128, 2, 8, 256)
            v = np.asarray(res["v_loc"])
            m = np.asarray(res["km_loc"])
            for i in range(8):
                g = zz_block(r, i)
                kt[:, :, g, :] = k[:, :, i, :]
                va[:, 2 * g:2 * g + 2] = v[:, 2 * i:2 * i + 2]
                km[:, :, g] = m[:, :, i]
        for r in range(4):
            extra[4 * bb + r].update({"kT_all": kt, "v_all": va, "km_all": km})
    return extra


def gather_out(res):
    return unshard_tokens([np.asarray(r["out"]) for r in res.results])


def run_all(inputs, final_norm=True, stages=7):
    ra = run({"stages": 2, "s5x": "emit"}, inputs)
    extra = relay_f(ra.results)
    rb = run({"stages": 4, "s5x": "input", "kvx": "emit"}, inputs, extra=extra)
    extra = relay_kv(rb.results, extra)
    rc = run({"stages": stages, "s5x": "input", "kvx": "input", "final_norm": final_norm}, inputs, extra=extra)
    return gather_out(rc), (ra, rb, rc)


def kernel(**inputs):
    res = run({"stages": 7, "s5x": "cc", "kvx": "cc", "final_norm": True}, inputs)
    return gather_out(res).astype(np.float32)
```

```python
import numpy as np
import concourse.bass as bass
import concourse.mybir as mybir
from concourse.bass_utils import run_bass_kernel_spmd

F32 = mybir.dt.float32
BF16 = mybir.dt.bfloat16
I32 = mybir.dt.int32
AF = mybir.ActivationFunctionType
ALU = mybir.AluOpType

D = 1024
NT = 2048
TB = 512
NTB = NT // TB
DFF = 2816
NCH = 8
NFF = 22
EPS = 1e-6
N_CORES = 8


def zz_block(r, i):
    return 8 * (i // 2) + (r if i % 2 == 0 else 7 - r)


class Op:
    __slots__ = ("eng", "idx", "fn", "deps", "dma", "marked", "sig", "inc")

    def __init__(self, eng, idx, fn, deps, dma):
        self.eng, self.idx, self.fn, self.deps, self.dma = eng, idx, fn, deps, dma
        self.marked = False
        self.sig = None
        self.inc = 16


class Sched:
    ENGS = ("pe", "act", "dve", "pool", "sp")

    def __init__(self, nc):
        self.nc = nc
        self.ops = {e: [] for e in self.ENGS}
        self.res = {}
        self.dma_cnt = {}

    @staticmethod
    def _merge(dct, dep):
        k = (dep[0], dep[1])
        if k not in dct or dct[k][2] < dep[2]:
            dct[k] = dep

    def op(self, eng, fn, r=(), w=(), dma=None, inc=16):
        lst = self.ops[eng]
        idx = len(lst)
        pr = [x for x in r if len(x) == 3 and x.startswith("ps")]
        if dma is not None:
            cnt = self.dma_cnt.get(dma, 0) + inc
            self.dma_cnt[dma] = cnt
            me = ("d", dma, cnt)
        else:
            me = ("c", eng, idx)
        deps = {}
        for x in r:
            st = self.res.setdefault(x, [None, {}])
            if st[0] is not None:
                self._merge(deps, st[0])
        for x in pr:
            for dp in self.res[x][1].values():
                if dp[0] == "c" and dp[1] != eng:
                    self._merge(deps, dp)
        for x in w:
            st = self.res.setdefault(x, [None, {}])
            if st[0] is not None:
                self._merge(deps, st[0])
            for dp in st[1].values():
                self._merge(deps, dp)
        for x in r:
            self._merge(self.res[x][1], me)
        for x in w:
            self.res[x][0] = me
            self.res[x][1] = {}
        if me[0] == "c" and eng == "pe":
            deps.pop(("c", "pe"), None)
        assert fn is not None or not w, "no-instruction ops cannot produce resources"
        o = Op(eng, idx, fn, list(deps.values()), dma)
        o.inc = inc
        if dma is not None:
            o.sig = me
        lst.append(o)
        return o

    def emit(self):
        nc = self.nc
        for e in self.ENGS:
            for o in self.ops[e]:
                for d in o.deps:
                    if d[0] == "c":
                        self.ops[d[1]][d[2]].marked = True
        sems = {}
        for e in self.ENGS:
            sems[("c", e)] = nc.alloc_semaphore(name="sem_" + e)
            cnt = 0
            for o in self.ops[e]:
                if o.dma is None and o.marked:
                    cnt += 1
                    o.sig = ("c", e, cnt)
        for k in self.dma_cnt:
            sems[("d", k)] = nc.alloc_semaphore(name="dsem_" + k.replace(".", "_"))
        self.nsem = len(sems)
        engobj = {"pe": "tensor", "act": "scalar", "dve": "vector", "pool": "gpsimd", "sp": "sync"}

        def replay(ename, e):
            waited = {}
            for o in self.ops[ename]:
                for d in o.deps:
                    if d[0] == "c":
                        tgt = self.ops[d[1]][d[2]].sig
                    else:
                        tgt = d
                    key = (tgt[0], tgt[1])
                    if waited.get(key, 0) < tgt[2]:
                        e.wait_ge(sems[key], tgt[2])
                        waited[key] = tgt[2]
                if o.fn is None:
                    continue
                ins = o.fn(e)
                if o.dma is not None:
                    ins.then_inc(sems[("d", o.dma)], o.inc)
                elif o.marked:
                    ins.then_inc(sems[("c", ename)], 1)

        with nc.Block() as block:
            @block.tensor
            def _(e):
                replay("pe", e)

            @block.scalar
            def _(e):
                replay("act", e)

            @block.vector
            def _(e):
                replay("dve", e)

            @block.gpsimd
            def _(e):
                replay("pool", e)

            @block.sync
            def _(e):
                replay("sp", e)


class Builder:
    def __init__(self, cfg):
        self.cfg = cfg
        self.nc = bass.Bass("TRN2", target_bir_lowering=False)
        self.S = Sched(self.nc)
        self.sb_bytes = 0
        self.ps_rr = 0
        self.din = {}
        self.dbg_keys = []

    def sb(self, name, shape, dt):
        n = 1
        for s in shape[1:]:
            n *= s
        self.sb_bytes += n * (4 if dt in (F32, I32) else 2)
        return self.nc.alloc_sbuf_tensor(name, list(shape), dt)

    def carve(self, shape, dt):
        n = 1
        for v in shape[1:]:
            n *= v
        esz = 4 if dt in (F32, I32) else 2
        off = self.arena_off
        nb = (n * esz + 63) // 64 * 64
        assert off + nb <= self.ARENA_BYTES, ("arena overflow", off, nb)
        self.arena_off = off + nb
        self.arena_max = max(self.arena_max, self.arena_off)
        v = self.arena[0:shape[0], off // 2: off // 2 + n * esz // 2]
        if dt != BF16:
            v = v.bitcast(dt)
        if len(shape) == 3:
            v = v.rearrange("p (a b) -> p a b", a=shape[1])
        elif len(shape) == 4:
            v = v.rearrange("p (a b c) -> p a b c", a=shape[1], b=shape[2])
        elif len(shape) == 5:
            v = v.rearrange("p (a b c d) -> p a b c d", a=shape[1], b=shape[2], c=shape[3])
        return v

    def phase(self, tag):
        S = self.S
        deps = {}
        for e in S.ENGS:
            if S.ops[e]:
                for o in reversed(S.ops[e]):
                    if o.dma is None and o.fn is not None:
                        deps[("c", e)] = ("c", e, o.idx)
                        break
        for k, cnt in S.dma_cnt.items():
            deps[("d", k)] = ("d", k, cnt)
        for e in S.ENGS:
            dl = [d for kk, d in deps.items() if not (kk == ("c", e) and e == "pe")]
            o = Op(e, len(S.ops[e]), None, dl, None)
            S.ops[e].append(o)
        self.arena_off = 0
        self.ptag = tag

    def dram_in(self, name, shape, dt=F32):
        t = self.nc.dram_tensor(name, list(shape), dt, kind="ExternalInput")
        self.din[name] = t
        return t.ap()

    def mm(self, out, lhsT, rhs, start, stop, r, w):
        return self.S.op("pe", lambda e: e.matmul(out, lhsT, rhs, start=start, stop=stop), r=r, w=w)

    def tr(self, out, in_, ident, r, w):
        return self.S.op("pe", lambda e: e.transpose(out, in_, ident), r=r, w=w)

    def act(self, out, in_, func, r, w, bias=None, scale=None, accum_out=None, eng="act"):
        kw = {}
        if bias is not None:
            kw["bias"] = bias
        if scale is not None:
            kw["scale"] = scale
        if accum_out is not None:
            kw["accum_out"] = accum_out
        return self.S.op(eng, lambda e: e.activation(out, in_, func, **kw), r=r, w=w)

    def tt(self, out, in0, in1, op, r, w, eng="dve"):
        return self.S.op(eng, lambda e: e.tensor_tensor(out, in0, in1, op), r=r, w=w)

    def ts(self, out, in0, s1, s2, op0, op1, r, w, eng="dve"):
        if op1 is None:
            return self.S.op(eng, lambda e: e.tensor_scalar(out, in0, s1, None, op0), r=r, w=w)
        return self.S.op(eng, lambda e: e.tensor_scalar(out, in0, s1, s2, op0, op1), r=r, w=w)

    def stt(self, out, in0, scalar, in1, op0, op1, r, w, eng="dve"):
        return self.S.op(eng, lambda e: e.scalar_tensor_tensor(out, in0, scalar, in1, op0, op1), r=r, w=w)

    def cp(self, out, in_, r, w, eng="dve"):
        if eng == "act":
            return self.S.op(eng, lambda e: e.copy(out, in_), r=r, w=w)
        return self.S.op(eng, lambda e: e.tensor_copy(out, in_), r=r, w=w)

    def memset(self, ap, val, w, eng="dve", r=()):
        return self.S.op(eng, lambda e: e.memset(ap, val), r=list(r), w=w)

    @staticmethod
    def bc(ap, axis, n):
        v = ap.unsqueeze(axis)
        shp = list(v.shape)
        shp[axis] = n
        return v.to_broadcast(shp)

    def carve_at(self, off, shape, dt):
        save = self.arena_off
        self.arena_off = off
        v = self.carve(shape, dt)
        end = self.arena_off
        self.arena_off = save
        return v, end

    def barrier(self):
        off = self.arena_off
        self.phase("b")
        self.arena_off = off

    def dbg_out(self, name, ap, shape, r):
        if name not in self.cfg.get("dbg", ()):
            return
        t = self.nc.dram_tensor("dbg_" + name, list(shape), F32, kind="ExternalOutput").ap()
        self.dma(t, ap, "dbg_" + name, r=r, w=["dbg_" + name], eng="pool")
        self.dbg_keys.append("dbg_" + name)

    def allgather(self, out_t, in_t, key, r, w):
        groups = [[0, 1, 2, 3], [4, 5, 6, 7]]
        return self.S.op("pool", lambda e: e.collective_compute(
            "AllGather", ALU.bypass, replica_groups=groups, ins=[in_t.ap().opt()], outs=[out_t.ap().opt()]),
            r=r, w=w, dma=key, inc=1)

    def dma(self, out, in_, key, r, w, eng="sp", **kw):
        return self.S.op(eng, lambda e: e.dma_start(out=out, in_=in_, **kw), r=r, w=w, dma=key)

    def setup_common(self):
        nc = self.nc
        self.x_in = self.dram_in("x", [NT, D])
        self.norm_g = self.dram_in("norm_g", [2, 3, D])
        sk = self.cfg.get("skip", ())
        if not (1 in sk and 3 in sk and self.cfg.get("stages", 99) < 5):
            self.w_in = self.dram_in("ffn_w_in", [2, 2, D, 2 * DFF])
            self.w_out = self.dram_in("ffn_w_out", [2, 2, DFF, D])
        self.final_g = self.dram_in("final_g", [D])
        self.out = nc.dram_tensor("out", [NT, D], F32, kind="ExternalOutput").ap()

        self.xT = self.sb("xT", [128, NCH, NT], F32)
        self.ARENA_BYTES = 105 * 1024
        self.arena = self.sb("arena", [128, self.ARENA_BYTES // 2], BF16)
        self.arena_off = 0
        self.arena_max = 0
        self.ident = self.sb("ident", [128, 128], F32)
        self.onesm = self.sb("onesm", [128, 128], BF16)
        self.iot = self.sb("iot", [128, 128], F32)
        self.ps = [nc.alloc_psum_tensor("ps%d" % i, [128, 512], F32) for i in range(8)]
        self.gcol = self.sb("gcol", [128, 9, NCH], F32)
        self.grow2 = self.sb("grow2", [8, 128], F32)
        self.grow = self.sb("grow", [64, 128], F32)
        self.wbuf = self.sb("wbuf", [128, 18432], BF16)
        self.win_s = [self.wbuf[:, i * 4096:(i + 1) * 4096].rearrange("p (k g n) -> p k g n", k=NCH, g=2)
                      for i in range(3)]
        self.wout_s = [self.wbuf[:, 12288 + i * 1536:12288 + (i + 1) * 1536].rearrange("p (j n) -> p j n", j=12)
                       for i in range(4)]
        self.identb2 = self.sb("identb2", [128, 128], BF16)
        self.attn_rr = 0
        self.win_rr = 0
        self.wout_rr = 0
        self.mm1_rr = 0
        self.mm2_rr = 0

        S = self.S
        iot, ident, onesm = self.iot, self.ident, self.onesm
        S.op("pool", lambda e: e.iota(iot[:, :], [[1, 128]], base=0, channel_multiplier=-1,
                                      allow_small_or_imprecise_dtypes=True), w=["iot"])
        self.ts(ident[:, :], iot[:, :], 0.0, None, ALU.is_equal, None, r=["iot"], w=["ident"])
        S.op("dve", lambda e: e.memset(onesm[:, :], 1.0 / 1024.0), w=["onesm"])
        self.cp(self.identb2[:, :], ident[:, :], r=["ident"], w=["identb2"])

        grow = self.grow
        S.op("dve", lambda e: e.memset(grow[:, :], 0.0), w=["grow"])
        ng = self.norm_g.rearrange("l n (c p) -> (l n c) p", p=128)
        self.dma(grow[0:48, :], ng, "grow", r=[], w=["grow"])
        fg = self.final_g.rearrange("(c p) -> c p", p=128)
        self.dma(grow[48:56, :], fg, "grow", r=[], w=["grow"])
        if self.cfg.get("stages", 99) >= 2 and 2 not in self.cfg.get("skip", ()):
            self.s5_setup()
            self.dma(grow[56:64, :], self.s5d[0].rearrange("(c p) -> c p", p=128), "grow", r=[], w=["grow"])
        self.tr(self.ps[7][:, 0:64], grow[:, :], ident[0:64, 0:64], r=["grow", "ident"], w=["ps7"])
        gc = self.gcol
        self.cp(gc[:, 0:8, :].rearrange("p n c -> p (n c)"), self.ps[7][:, 0:64], r=["ps7"], w=["gcol"])
        if self.cfg.get("stages", 99) >= 4:
            self.kvg = self.dram_in("kv_norm_g", [D])
            self.dma(self.grow2[:, :], self.kvg.rearrange("(c p) -> c p", p=128), "grow2", r=[], w=["grow2"])
            self.tr(self.ps[7][:, 64:72], self.grow2[:, :], ident[0:8, 0:8], r=["grow2", "ident"], w=["ps7"])
            self.cp(gc[:, 8, :], self.ps[7][:, 64:72], r=["ps7"], w=["gcol"])

    def load_x(self):
        self.phase("load")
        self.stage = [self.carve([128, D], F32) for i in range(2)]
        for tt in range(NT // 128):
            st = self.stage[tt % 2]
            sn = "stage%d" % (tt % 2)
            self.dma(st[:, :], self.x_in[tt * 128:(tt + 1) * 128, :], sn, r=[], w=[sn])
            for half in range(2):
                b = 5 + ((2 * tt + half) % 2)
                pn = "ps%d" % b
                for q in range(4):
                    c = half * 4 + q
                    self.tr(self.ps[b][:, q * 128:(q + 1) * 128], st[:, c * 128:(c + 1) * 128],
                            self.ident[:, :], r=[sn, "ident"], w=[pn])
                dst = self.xT[:, half * 4:half * 4 + 4, tt * 128:(tt + 1) * 128]
                src = self.ps[b][:, :].rearrange("p (q t) -> p q t", q=4)
                wr = ["xT.%d.%d" % (half * 4 + q, tt // 4) for q in range(4)]
                self.cp(dst, src, r=[pn], w=wr, eng=("act" if half == 0 else "dve"))

    def store_x(self, final_norm):
        self.phase("store")
        self.stage = [self.carve([128, D], F32) for i in range(2)]
        if final_norm:
            self.gfin = self.carve([128, D], F32)
            self.dma(self.gfin[:, :], self.final_g.partition_broadcast(128), "gfin", r=[], w=["gfin"])
            self.ssum = self.carve([128, 2], F32)
            self.junk = self.carve([128, D], BF16)
        for tt in range(NT // 128):
            st = self.stage[tt % 2]
            sn = "stage%d" % (tt % 2)
            for half in range(2):
                b = 5 + ((2 * tt + half) % 2)
                pn = "ps%d" % b
                for q in range(4):
                    c = half * 4 + q
                    self.tr(self.ps[b][:, q * 128:(q + 1) * 128], self.xT[:, c, tt * 128:(tt + 1) * 128],
                            self.ident[:, :], r=["xT.%d.%d" % (c, tt // 4), "ident"], w=[pn])
                self.cp(st[:, half * 512:(half + 1) * 512], self.ps[b][:, :], r=[pn], w=[sn],
                        eng=("act" if half == 0 else "dve"))
            if final_norm:
                ss = self.ssum
                k = tt % 2
                self.act(self.junk[:, :], st[:, :], AF.Square, r=[sn], w=["junk", "ssum%d" % k],
                         accum_out=ss[:, k:k + 1])
                self.ts(ss[:, k:k + 1], ss[:, k:k + 1], 1.0 / 1024.0, EPS, ALU.mult, ALU.add,
                        r=["ssum%d" % k], w=["ssum%d" % k])
                self.act(ss[:, k:k + 1], ss[:, k:k + 1], AF.Sqrt, r=["ssum%d" % k], w=["ssum%d" % k])
                self.S.op("dve", lambda e, o=ss[:, k:k + 1]: e.reciprocal(o, o),
                          r=["ssum%d" % k], w=["ssum%d" % k])
                self.stt(st[:, :], st[:, :], ss[:, k:k + 1], self.gfin[:, :], ALU.mult, ALU.mult,
                         r=[sn, "ssum%d" % k, "gfin"], w=[sn])
            self.dma(self.out[tt * 128:(tt + 1) * 128, :], st[:, :], "out%d" % (tt % 2), r=[sn], w=["out.%d" % (tt % 2)])
        self.S.op("sp", None, r=["out.0", "out.1"], w=[])

    def rmsnorm(self, n):
        g = self.gcol
        for t in range(NTB):
            tb = slice(t * TB, (t + 1) * TB)
            sq = self.sq
            sqn = "sq"
            for c in range(NCH):
                self.act(sq[:, c, :], self.xT[:, c, tb], AF.Square, r=["xT.%d.%d" % (c, t)], w=[sqn + ".%d" % c])
            for c in range(NCH):
                self.mm(self.ps[4][:, :], self.onesm[:, :], sq[:, c, :], c == 0, c == NCH - 1,
                        r=["onesm", sqn + ".%d" % c], w=["ps4"])
            rs = self.rstd[t % 2]
            rn = "rstd%d" % (t % 2)
            self.ts(rs[:, :], self.ps[4][:, :], EPS, None, ALU.add, None, r=["ps4"], w=[rn])
            self.act(rs[:, :], rs[:, :], AF.Sqrt, r=[rn], w=[rn])
            self.S.op("dve", lambda e, o=rs[:, :]: e.reciprocal(o, o), r=[rn], w=[rn])
            for c in range(NCH):
                self.stt(self.hT[:, c, tb], self.xT[:, c, tb], g[:, n, c:c + 1], rs[:, :], ALU.mult, ALU.mult,
                         r=["xT.%d.%d" % (c, t), "gcol", rn], w=["hT.%d.%d" % (c, t)])

    def ffn(self, l, i, n):
        self.phase("ffn%d%d" % (l, i))
        self.hT = self.carve([128, NCH, NT], BF16)
        self.actT = self.carve([128, 12, NT], BF16)
        self.sg = [self.carve([128, TB], F32) for _ in range(2)]
        self.sq = self.carve([128, NCH, TB], BF16)
        self.rstd = [self.carve([128, TB], F32) for _ in range(2)]
        self.rmsnorm(n)
        win = self.w_in[l, i].rearrange("(k p) f -> p k f", p=128)
        wout = self.w_out[l, i].rearrange("(j p) n -> p j n", p=128)
        for (j0, nj) in ((0, 12), (12, 10)):
            for gs in range(nj // 2):
                s = self.win_rr % 3
                self.win_rr += 1
                wsl = self.win_s[s]
                wn = "win%d" % s
                c0 = (j0 + 2 * gs) * 128
                self.dma(wsl[:, :, 0, :], win[:, :, c0:c0 + 256], wn + "g", r=[], w=[wn + "g"], eng="pool")
                self.dma(wsl[:, :, 1, :], win[:, :, DFF + c0:DFF + c0 + 256], wn + "u", r=[], w=[wn + "u"], eng="pool")
                for jl in range(2):
                    jj = 2 * gs + jl
                    for t in range(NTB):
                        tb = slice(t * TB, (t + 1) * TB)
                        pa = 2 * (self.mm1_rr % 2)
                        self.mm1_rr += 1
                        for gu in range(2):
                            for k in range(NCH):
                                self.mm(self.ps[pa + gu][:, :], wsl[:, k, gu, jl * 128:(jl + 1) * 128],
                                        self.hT[:, k, tb], k == 0, k == NCH - 1,
                                        r=[wn + "gu"[gu], "hT.%d.%d" % (k, t)], w=["ps%d" % (pa + gu)])
                        sg = self.sg[(pa // 2)]
                        sgn = "sg%d" % (pa // 2)
                        self.act(sg[:, :], self.ps[pa][:, :], AF.Silu, r=["ps%d" % pa], w=[sgn])
                        self.tt(self.actT[:, jj, tb], sg[:, :], self.ps[pa + 1][:, :], ALU.mult,
                                r=[sgn, "ps%d" % (pa + 1)], w=["actT.%d.%d" % (jj, t)])
            for c in range(NCH):
                s = self.wout_rr % 4
                self.wout_rr += 1
                wsl = self.wout_s[s]
                wn = "wout%d" % s
                self.dma(wsl[:, 0:nj, :], wout[:, j0:j0 + nj, c * 128:(c + 1) * 128], wn, r=[], w=[wn], eng="pool")
                for t in range(NTB):
                    tb = slice(t * TB, (t + 1) * TB)
                    pb = 4 + (self.mm2_rr % 4)
                    self.mm2_rr += 1
                    for jj in range(nj):
                        self.mm(self.ps[pb][:, :], wsl[:, jj, :], self.actT[:, jj, tb], jj == 0, jj == nj - 1,
                                r=[wn, "actT.%d.%d" % (jj, t)], w=["ps%d" % pb])
                    self.stt(self.xT[:, c, tb], self.ps[pb][:, :], 0.5, self.xT[:, c, tb], ALU.mult, ALU.add,
                             r=["ps%d" % pb, "xT.%d.%d" % (c, t)], w=["xT.%d.%d" % (c, t)])


    def s5_setup(self):
        nc, S = self.nc, self.S
        self.a_re = self.dram_in("s5_a_re", [1, 64, 64])
        self.a_im = self.dram_in("s5_a_im", [1, 64, 64])
        self.lstep = self.dram_in("s5_log_step", [1, 64])
        self.b_re = self.dram_in("s5_b_re", [1, 64, 64, 16])
        self.b_im = self.dram_in("s5_b_im", [1, 64, 64, 16])
        self.c_re = self.dram_in("s5_c_re", [1, 64, 16, 64])
        self.c_im = self.dram_in("s5_c_im", [1, 64, 16, 64])
        self.s5d = self.dram_in("s5_d", [1, D])
        self.wglu = self.dram_in("s5_w_glu", [1, D, 2 * D])

    def s5_tables(self):
        NK = 34
        self.tabre = self.carve([128, 32, NK], F32)
        self.tabim = self.carve([128, 32, NK], F32)
        self.bm = [self.carve([128, 32, 2, 16], F32) for _ in range(2)]
        self.bmb = [self.carve([128, 32, 2, 16], BF16) for _ in range(2)]
        self.identb = self.carve([128, 128], BF16)
        self.rowpar = self.carve([128, 2], F32)
        self.dcol = self.carve([128, NCH], F32)
        keep = self.arena_off
        tl = self.carve([32, 3, 128], F32)
        ls2 = self.carve([32, 2], F32)
        sc = self.carve([128, 3, 32], F32)
        kv = self.carve([128, NK], F32)
        t = [self.carve([128, 32, NK], F32) for _ in range(4)]
        ti = self.carve([128, 32, NK], I32)
        cf = self.carve([128, 8, 32], F32)
        bld = [self.carve([128, 32, 16], F32) for _ in range(2)]
        bp = [self.carve([128, 32, 16], F32) for _ in range(2)]
        pm = self.carve([128, 2], F32)
        ipar = self.carve([128, 1], I32)
        S = self.S
        self.cp(self.identb[:, :], self.ident[:, :], r=["ident"], w=["identb"])
        self.dma(tl[:, 0, :], self.a_re[0].rearrange("(q two) p -> q (two p)", two=2), "tl0", r=[], w=["tl0"])
        self.dma(tl[:, 1, :], self.a_im[0].rearrange("(q two) p -> q (two p)", two=2), "tl1", r=[], w=["tl1"])
        self.dma(ls2[:, :], self.lstep[0].rearrange("(q two) -> q two", two=2), "ls2", r=[], w=["ls2"])
        self.cp(tl[:, 2, :].rearrange("q (two p) -> q two p", two=2), self.bc(ls2[:, :], 2, 64),
                r=["ls2"], w=["tl2"])
        for i in range(3):
            self.tr(self.ps[5][:, i * 32:(i + 1) * 32], tl[:, i, :], self.ident[0:32, 0:32],
                    r=["tl%d" % i, "ident"], w=["ps5"])
        self.cp(sc[:, :, :], self.ps[5][:, 0:96].rearrange("p (a b) -> p a b", a=3), r=["ps5"], w=["sc"])
        self.act(sc[:, 2, :], sc[:, 2, :], AF.Exp, r=["sc"], w=["sc"])
        for h in range(2):
            for (dst, src, nm) in ((bld[0], self.b_re, "bld0"), (bld[1], self.b_im, "bld1")):
                self.dma(dst[64 * h:64 * h + 64, :, :],
                         src[0].rearrange("(q two) p c -> two p q c", two=2)[h],
                         nm + "h%d" % h, r=[], w=[nm + "h%d" % h])
        bldr = [["bld0h0", "bld0h1"], ["bld1h0", "bld1h1"]]
        S.op("pool", lambda e: e.iota(kv[:, 0:17], [[1, 17]], base=0, channel_multiplier=0,
                                      allow_small_or_imprecise_dtypes=True), w=["kv"])
        S.op("pool", lambda e: e.iota(kv[:, 17:33], [[16, 16]], base=0, channel_multiplier=0,
                                      allow_small_or_imprecise_dtypes=True), w=["kv"])
        self.memset(kv[:, 33:34], 256.0, w=["kv"], eng="pool")
        self.tt(cf[:, 0, :], sc[:, 0, :], sc[:, 2, :], ALU.mult, r=["sc"], w=["cf0"])
        self.tt(cf[:, 1, :], sc[:, 1, :], sc[:, 2, :], ALU.mult, r=["sc"], w=["cf1"])
        kvb = self.bc(kv[:, :], 1, 32)
        self.tt(t[0][:, :, :], self.bc(cf[:, 0, :], 2, NK), kvb, ALU.mult, r=["cf0", "kv"], w=["t0"])
        self.act(t[0][:, :, :], t[0][:, :, :], AF.Exp, r=["t0"], w=["t0"])
        self.tt(t[1][:, :, :], self.bc(cf[:, 1, :], 2, NK), kvb, ALU.mult, r=["cf1", "kv"], w=["t1"])
        self.ts(t[1][:, :, :], t[1][:, :, :], 1.0 / (2 * np.pi), None, ALU.mult, None, r=["t1"], w=["t1"])
        SC = 2 * np.pi * (1.0 - 1e-6)
        for (dst, off, nm) in ((self.tabim, 0.0, "tabim"), (self.tabre, 0.25, "tabre")):
            if off:
                self.ts(t[2][:, :, :], t[1][:, :, :], off, None, ALU.add, None, r=["t1"], w=["t2"])
            else:
                self.cp(t[2][:, :, :], t[1][:, :, :], r=["t1"], w=["t2"])
            self.cp(ti[:, :, :], t[2][:, :, :], r=["t2"], w=["ti"])
            self.cp(t[3][:, :, :], ti[:, :, :], r=["ti"], w=["t3"])
            self.tt(t[2][:, :, :], t[2][:, :, :], t[3][:, :, :], ALU.subtract, r=["t2", "t3"], w=["t2"])
            self.act(t[2][:, :, :], t[2][:, :, :], AF.Sin, r=["t2"], w=["t2"], scale=SC)
            self.tt(dst[:, :, :], t[2][:, :, :], t[0][:, :, :], ALU.mult, r=["t2", "t0"], w=[nm])
        lr, li = sc[:, 0, :], sc[:, 1, :]
        are, aim = self.tabre[:, :, 1], self.tabim[:, :, 1]
        self.ts(cf[:, 2, :], are, -1.0, None, ALU.add, None, r=["tabre"], w=["cf2"])
        self.tt(cf[:, 3, :], lr, lr, ALU.mult, r=["sc"], w=["cf3"])
        self.tt(cf[:, 4, :], li, li, ALU.mult, r=["sc"], w=["cf4"])
        self.tt(cf[:, 3, :], cf[:, 3, :], cf[:, 4, :], ALU.add, r=["cf3", "cf4"], w=["cf3"])
        S.op("dve", lambda e: e.reciprocal(cf[:, 3, :], cf[:, 3, :]), r=["cf3"], w=["cf3"])
        self.tt(cf[:, 4, :], cf[:, 2, :], lr, ALU.mult, r=["cf2", "sc"], w=["cf4"])
        self.tt(cf[:, 5, :], aim, li, ALU.mult, r=["tabim", "sc"], w=["cf5"])
        self.tt(cf[:, 4, :], cf[:, 4, :], cf[:, 5, :], ALU.add, r=["cf4", "cf5"], w=["cf4"])
        self.tt(cf[:, 4, :], cf[:, 4, :], cf[:, 3, :], ALU.mult, r=["cf4", "cf3"], w=["cf4"])
        self.tt(cf[:, 5, :], aim, lr, ALU.mult, r=["tabim", "sc"], w=["cf5"])
        self.tt(cf[:, 6, :], cf[:, 2, :], li, ALU.mult, r=["cf2", "sc"], w=["cf6"])
        self.tt(cf[:, 5, :], cf[:, 5, :], cf[:, 6, :], ALU.subtract, r=["cf5", "cf6"], w=["cf5"])
        self.tt(cf[:, 5, :], cf[:, 5, :], cf[:, 3, :], ALU.mult, r=["cf5", "cf3"], w=["cf5"])
        cre = self.bc(cf[:, 4, :], 2, 16)
        cim = self.bc(cf[:, 5, :], 2, 16)
        self.tt(bp[0][:, :, :], bld[0][:, :, :], cre, ALU.mult, r=bldr[0] + ["cf4"], w=["bp0"])
        self.tt(t[0][:, :, 0:16], bld[1][:, :, :], cim, ALU.mult, r=bldr[1] + ["cf5"], w=["t0"])
        self.tt(bp[0][:, :, :], bp[0][:, :, :], t[0][:, :, 0:16], ALU.subtract, r=["bp0", "t0"], w=["bp0"])
        self.tt(bp[1][:, :, :], bld[1][:, :, :], cre, ALU.mult, r=bldr[1] + ["cf4"], w=["bp1"])
        self.tt(t[1][:, :, 0:16], bld[0][:, :, :], cim, ALU.mult, r=bldr[0] + ["cf5"], w=["t1"])
        self.tt(bp[1][:, :, :], bp[1][:, :, :], t[1][:, :, 0:16], ALU.add, r=["bp1", "t1"], w=["bp1"])
        self.memset(pm[:, :], 0.0, w=["pm"])
        self.memset(pm[0:64, 0:1], 1.0, w=["pm"])
        self.memset(pm[64:128, 1:2], 1.0, w=["pm"])
        for ri in range(2):
            in0 = self.bc(bp[ri][:, :, :], 2, 2)
            in1 = self.bc(self.bc(pm[:, :], 1, 32), 3, 16)
            self.tt(self.bm[ri][:, :, :, :], in0, in1, ALU.mult, r=["bp%d" % ri, "pm"], w=["bm%d" % ri])
            self.cp(self.bmb[ri][:, :, :, :], self.bm[ri][:, :, :, :], r=["bm%d" % ri], w=["bmb%d" % ri])
        S.op("pool", lambda e: e.iota(ipar[:, :], [[0, 1]], base=0, channel_multiplier=1), w=["ipar"])
        S.op("dve", lambda e: e.tensor_scalar(ipar[:, :], ipar[:, :], 4, 1, ALU.arith_shift_right, ALU.bitwise_and),
             r=["ipar"], w=["ipar"])
        self.cp(self.rowpar[:, 1:2], ipar[:, :], r=["ipar"], w=["rowpar"])
        self.ts(self.rowpar[:, 0:1], self.rowpar[:, 1:2], -1.0, 1.0, ALU.mult, ALU.add, r=["rowpar"], w=["rowpar"])
        self.cp(self.dcol[:, :], self.gcol[:, 7, :], r=["gcol"], w=["dcol"])
        self.dbg_out("tabre", self.tabre[:, :, :], [128, 32, NK], r=["tabre"])
        self.dbg_out("tabim", self.tabim[:, :, :], [128, 32, NK], r=["tabim"])
        self.dbg_out("bm0", self.bm[0][:, :, :, :], [128, 32, 2, 16], r=["bm0"])
        self.dbg_out("bm1", self.bm[1][:, :, :, :], [128, 32, 2, 16], r=["bm1"])
        self.barrier()
        self.arena_off = keep

    def s5_pass_a(self):
        S = self.S
        self.X_off = self.arena_off
        self.X = self.carve([128, 32, 2, 128], F32)
        keepX = self.arena_off
        wx = self.carve([128, 2, 16, 128], BF16)
        tmp = [self.carve([128, 8, 4, 32], F32) for _ in range(2)]
        mz = [self.carve([128, 8, 128], BF16) for _ in range(2)]
        psb = [self.ps[5][:, :].bitcast(BF16), self.ps[6][:, :].bitcast(BF16)]
        trr = 0
        for ct in range(NCH):
            p0 = 4 * ct
            for kh in range(2):
                ks = slice(kh * 8, kh * 8 + 8)
                bmr = self.bc(self.bm[0][:, p0:p0 + 4, :, :].rearrange("p g a c -> p g (a c)"), 1, 8)
                bmi = self.bc(self.bm[1][:, p0:p0 + 4, :, :].rearrange("p g a c -> p g (a c)"), 1, 8)
                er = self.bc(self.tabre[:, p0:p0 + 4, ks].rearrange("p g k -> p k g"), 3, 32)
                ei = self.bc(self.tabim[:, p0:p0 + 4, ks].rearrange("p g k -> p k g"), 3, 32)
                mzr = mz[0][:, :, :].rearrange("p k (g c) -> p k g c", g=4)
                mzi = mz[1][:, :, :].rearrange("p k (g c) -> p k g c", g=4)
                self.tt(tmp[0][:, :, :, :], bmr, er, ALU.mult, r=["bm0", "tabre"], w=["tmp0"])
                self.tt(tmp[1][:, :, :, :], bmi, ei, ALU.mult, r=["bm1", "tabim"], w=["tmp1"])
                self.tt(mzr, tmp[0][:, :, :, :], tmp[1][:, :, :, :], ALU.subtract, r=["tmp0", "tmp1"], w=["mz0"])
                self.tt(tmp[0][:, :, :, :], bmr, ei, ALU.mult, r=["bm0", "tabim"], w=["tmp0"])
                self.tt(tmp[1][:, :, :, :], bmi, er, ALU.mult, r=["bm1", "tabre"], w=["tmp1"])
                self.tt(mzi, tmp[0][:, :, :, :], tmp[1][:, :, :, :], ALU.add, r=["tmp0", "tmp1"], w=["mz1"])
                for ri in range(2):
                    for q4 in range(2):
                        pb = trr % 2
                        trr += 1
                        pn = "ps%d" % (5 + pb)
                        for q in range(4):
                            k = q4 * 4 + q
                            self.tr(psb[pb][:, q * 128:(q + 1) * 128], mz[ri][:, k, :], self.identb[:, :],
                                    r=["mz%d" % ri, "identb", "xdone"], w=[pn])
                        k0 = kh * 8 + q4 * 4
                        self.cp(wx[:, ri, k0:k0 + 4, :], psb[pb][:, 0:512].rearrange("p (k n) -> p k n", k=4),
                                r=[pn], w=["wx"], eng="act")
            uv = self.hT[:, ct, :].rearrange("p (n i) -> p n i", i=16)
            for ri in range(2):
                for k in range(16):
                    ip = 15 - k
                    for g in range(4):
                        pr = slice(32 * g, 32 * g + 32)
                        S.op("pe", lambda e, g=g, ri=ri, k=k, ip=ip, pr=pr, uv=uv: e.matmul(
                            self.ps[g][:, ri * 128:(ri + 1) * 128], wx[pr, ri, k, :], uv[pr, :, ip],
                            start=(k == 0), stop=(k == 15), tile_position=(32 * g, 0)),
                            r=["wx", "hT.%d.0" % ct, "hT.%d.1" % ct, "hT.%d.2" % ct, "hT.%d.3" % ct],
                            w=["ps%d" % g])
            for g in range(4):
                self.cp(self.X[:, p0 + g, :, :], self.ps[g][:, 0:256].rearrange("p (r n) -> p r n", r=2),
                        r=["ps%d" % g], w=["X", "xdone"], eng="act")
        self.dbg_out("X", self.X[:, :, :, :], [128, 32, 2, 128], r=["X"])
        self.barrier()
        self.arena_off = keepX
        t = [self.carve([128, 32, 8], F32) for _ in range(4)]
        X5 = self.X[:, :, :, :].rearrange("p t r (b j) -> p t r b j", j=16)
        a16r = self.bc(self.tabre[:, :, 16], 2, 8)
        a16i = self.bc(self.tabim[:, :, 16], 2, 8)
        for j in range(1, 16):
            pr, pi = X5[:, :, 0, :, j - 1], X5[:, :, 1, :, j - 1]
            cr, ci = X5[:, :, 0, :, j], X5[:, :, 1, :, j]
            rp = ["X.%d" % (j - 1)]
            self.tt(t[0][:, :, :], pr, a16r, ALU.mult, r=rp, w=["sc0"])
            self.tt(t[1][:, :, :], pi, a16i, ALU.mult, r=rp, w=["sc1"])
            self.tt(t[2][:, :, :], pr, a16i, ALU.mult, r=rp, w=["sc2"])
            self.tt(t[3][:, :, :], pi, a16r, ALU.mult, r=rp, w=["sc3"])
            self.tt(cr, cr, t[0][:, :, :], ALU.add, r=["sc0"], w=["X.%d" % j])
            self.tt(cr, cr, t[1][:, :, :], ALU.subtract, r=["sc1"], w=["X.%d" % j])
            self.tt(ci, ci, t[2][:, :, :], ALU.add, r=["sc2"], w=["X.%d" % j])
            self.tt(ci, ci, t[3][:, :, :], ALU.add, r=["sc3"], w=["X.%d" % j])
        self.barrier()
        self.arena_off = keepX

    def s5_exchange(self):
        nc = self.nc
        X5 = self.X[:, :, :, :].rearrange("p t r (b j) -> p t r b j", j=16)
        mode = self.cfg.get("s5x", "zero")
        self.Pown = self.carve([128, 32, 2, 8], F32)
        keep = self.arena_off
        if mode == "zero":
            self.memset(self.Pown[:, :, :, :], 0.0, w=["Pown"])
            return
        if mode == "emit":
            f = nc.dram_tensor("f_out", [128, 32, 2, 8], F32, kind="ExternalOutput").ap()
            fs = self.carve([128, 32, 2, 8], F32)
            self.cp(fs[:, :, :, :], X5[:, :, :, :, 15], r=["X.15"], w=["fs"])
            self.dma(f, fs[:, :, :, :], "f_out", r=["fs"], w=["f_out"])
            self.S.op("sp", None, r=["f_out"], w=[])
            return
        sel_d = self.dram_in("blk_sel", [8, 32])
        if mode == "cc":
            f_loc = nc.dram_tensor("f_loc", [128, 512], F32)
            f_gath = nc.dram_tensor("f_gath", [512, 512], F32)
            fs = self.carve([128, 32, 2, 8], F32)
            self.cp(fs[:, :, :, :], X5[:, :, :, :, 15], r=["X.15"], w=["fs"])
            self.dma(f_loc.ap(), fs[:, :, :, :].rearrange("p t r b -> p (t r b)"), "f_loc", r=["fs"], w=["f_loc"])
            self.allgather(f_gath, f_loc, "cc_f", r=["f_loc"], w=["f_gath"])
            fg = self.wbuf[:, 0:4096].bitcast(F32).rearrange("p (q t r b) -> p q t r b", q=4, t=32, r=2)
            self.dma(fg, f_gath.ap().rearrange("(q p) (t r b) -> p q t r b", q=4, t=32, r=2), "fg", r=["f_gath"], w=["fall"])
            inv = {}
            for rr_ in range(4):
                for ii in range(8):
                    inv[zz_block(rr_, ii)] = (rr_, ii)
            fsrc = lambda ri, b: fg[:, inv[b][0], :, ri, inv[b][1]]
        else:
            fall_d = self.dram_in("f_all", [128, 32, 2, 32])
        fall = self.wbuf[:, 0:4096].bitcast(F32).rearrange("p (t r b) -> p t r b", t=32, r=2)
        pall = self.wbuf[:, 4096:8192].bitcast(F32).rearrange("p (t r b) -> p t r b", t=32, r=2)
        tsel = self.wbuf[:, 8192:12288].bitcast(F32).rearrange("p (a b) -> p a b", a=64)
        sel = self.carve([128, 8, 32], F32)
        t = [self.carve([128, 32], F32) for _ in range(4)]
        if mode != "cc":
            self.dma(fall[:, :, :, :], fall_d, "fall", r=[], w=["fall"])
            fsrc = lambda ri, b: fall[:, :, ri, b]
        self.dma(sel[:, :, :], sel_d.partition_broadcast(128), "sel", r=[], w=["sel"])
        self.memset(pall[:, :, :, 0], 0.0, w=["pall.0"])
        Ar, Ai = self.tabre[:, :, 33], self.tabim[:, :, 33]
        for b in range(1, 32):
            pr, pi = pall[:, :, 0, b - 1], pall[:, :, 1, b - 1]
            rp = ["pall.%d" % (b - 1)]
            self.tt(t[0][:, :], pr, Ar, ALU.mult, r=rp, w=["px0"])
            self.tt(t[1][:, :], pi, Ai, ALU.mult, r=rp, w=["px1"])
            self.tt(t[2][:, :], pr, Ai, ALU.mult, r=rp, w=["px2"])
            self.tt(t[3][:, :], pi, Ar, ALU.mult, r=rp, w=["px3"])
            self.tt(t[0][:, :], t[0][:, :], t[1][:, :], ALU.subtract, r=["px0", "px1"], w=["px0"])
            self.tt(pall[:, :, 0, b], t[0][:, :], fsrc(0, b - 1), ALU.add, r=["px0", "fall"], w=["pall.%d" % b])
            self.tt(t[2][:, :], t[2][:, :], t[3][:, :], ALU.add, r=["px2", "px3"], w=["px2"])
            self.tt(pall[:, :, 1, b], t[2][:, :], fsrc(1, b - 1), ALU.add, r=["px2", "fall"], w=["pall.%d" % b])
        pv = pall[:, :, :, :].rearrange("p t r b -> p (t r) b")
        allp = ["pall.%d" % b for b in range(32)]
        for i in range(8):
            self.tt(tsel[:, :, :], pv, self.bc(sel[:, i, :], 1, 64), ALU.mult, r=allp + ["sel"], w=["tsel"])
            self.S.op("dve", lambda e, i=i: e.tensor_reduce(
                self.Pown[:, :, :, i].rearrange("p t r -> p (t r)"), tsel[:, :, :],
                mybir.AxisListType.X, ALU.add), r=["tsel"], w=["Pown"])
        self.barrier()
        self.arena_off = keep

    def s5_pass_b(self):
        S = self.S
        X5 = self.X[:, :, :, :].rearrange("p t r (b j) -> p t r b j", j=16)
        sprev = self.carve([128, 32, 2, 128], BF16)
        sp5 = sprev[:, :, :, :].rearrange("p t r (b j) -> p t r b j", j=16)
        keep = self.arena_off
        tA = self.wout_s[0][:, :, :].rearrange("p a b -> p (a b)").bitcast(F32)[:, 0:512].rearrange("p (a b) -> p a b", a=32)
        tB = self.wout_s[1][:, :, :].rearrange("p a b -> p (a b)").bitcast(F32)[:, 0:512].rearrange("p (a b) -> p a b", a=32)
        Tre, Tim = self.tabre[:, :, 17:33], self.tabim[:, :, 17:33]
        allX = ["X.%d" % j for j in range(16)]
        for b in range(8):
            pre = self.bc(self.Pown[:, :, 0, b], 2, 16)
            pim = self.bc(self.Pown[:, :, 1, b], 2, 16)
            for ri in range(2):
                if ri == 0:
                    self.tt(tA[:, :, :], Tre, pre, ALU.mult, r=["tabre", "Pown"], w=["tA"])
                    self.tt(tB[:, :, :], Tim, pim, ALU.mult, r=["tabim", "Pown"], w=["tB"])
                    self.tt(tA[:, :, :], tA[:, :, :], tB[:, :, :], ALU.subtract, r=["tA", "tB"], w=["tA"])
                else:
                    self.tt(tA[:, :, :], Tre, pim, ALU.mult, r=["tabre", "Pown"], w=["tA"])
                    self.tt(tB[:, :, :], Tim, pre, ALU.mult, r=["tabim", "Pown"], w=["tB"])
                    self.tt(tA[:, :, :], tA[:, :, :], tB[:, :, :], ALU.add, r=["tA", "tB"], w=["tA"])
                self.cp(sp5[:, :, ri, b, 0], tA[:, :, 0], r=["tA"], w=["sprev"])
                self.tt(sp5[:, :, ri, b, 1:16], tA[:, :, 1:16], X5[:, :, ri, b, 0:15], ALU.add,
                        r=["tA"] + allX, w=["sprev"])
        self.dbg_out("sprev", sprev[:, :, :, :], [128, 32, 2, 128], r=["sprev"])
        self.barrier()
        self.arena_off = self.X_off
        cld = [self.carve([128, 64], F32) for _ in range(2)]
        cm = [self.carve([128, 2, 64], F32) for _ in range(2)]
        ctt = [self.carve([128, 128], F32) for _ in range(2)]
        wy = self.carve([128, 4, 2, 17, 32], BF16)
        kw = self.carve([128, 16, 128], BF16)
        tA = self.carve([128, 2, 17, 32], F32)
        tB = self.carve([128, 2, 17, 32], F32)
        ytmp = [self.carve([128, 128, 4], F32) for _ in range(2)]
        self.memset(kw[:, :, :], 0.0, w=["kw"])
        cre_d = self.c_re[0].rearrange("(ct gl) c p -> ct (gl c) p", ct=8)
        cim_d = self.c_im[0].rearrange("(ct gl) c p -> ct (gl c) p", ct=8)
        for ct in range(NCH):
            p0 = 4 * ct
            for ri, src in ((0, cre_d), (1, cim_d)):
                self.dma(cld[ri][:, :], src[ct], "cld%d" % ri, r=[], w=["cld%d" % ri])
                self.tt(cm[ri][:, :, :], self.bc(cld[ri][:, :], 1, 2), self.bc(self.rowpar[:, :], 2, 64), ALU.mult,
                        r=["cld%d" % ri, "rowpar"], w=["cm%d" % ri])
                self.tr(self.ps[5][:, ri * 128:(ri + 1) * 128], cm[ri][:, :, :].rearrange("p a b -> p (a b)"),
                        self.ident[:, :], r=["cm%d" % ri, "ident"], w=["ps5"])
                self.cp(ctt[ri][:, :], self.ps[5][:, ri * 128:(ri + 1) * 128], r=["ps5"], w=["ctt%d" % ri], eng="act")
            for gh in range(2):
                gsl = slice(2 * gh, 2 * gh + 2)
                cr = self.bc(ctt[0][:, :].rearrange("p (g c) -> p g c", g=4)[:, gsl, :], 2, 17)
                ci = self.bc(ctt[1][:, :].rearrange("p (g c) -> p g c", g=4)[:, gsl, :], 2, 17)
                er = self.bc(self.tabre[:, p0 + 2 * gh:p0 + 2 * gh + 2, 0:17], 3, 32)
                ei = self.bc(self.tabim[:, p0 + 2 * gh:p0 + 2 * gh + 2, 0:17], 3, 32)
                self.tt(tA[:, :, :, :], cr, er, ALU.mult, r=["ctt0", "tabre"], w=["wtA"])
                self.tt(tB[:, :, :, :], ci, ei, ALU.mult, r=["ctt1", "tabim"], w=["wtB"])
                self.tt(wy[:, gsl, 0, :, :], tA[:, :, :, :], tB[:, :, :, :], ALU.subtract, r=["wtA", "wtB"], w=["wy0"])
                self.tt(tA[:, :, :, :], cr, ei, ALU.mult, r=["ctt0", "tabim"], w=["wtA"])
                self.tt(tB[:, :, :, :], ci, er, ALU.mult, r=["ctt1", "tabre"], w=["wtB"])
                self.stt(wy[:, gsl, 1, :, :], tA[:, :, :, :], -1.0, tB[:, :, :, :], ALU.mult, ALU.subtract,
                         r=["wtA", "wtB"], w=["wy1"])
            for g in range(4):
                pr = slice(32 * g, 32 * g + 32)
                for ri in range(2):
                    S.op("pe", lambda e, g=g, ri=ri, pr=pr, p0=p0: e.matmul(
                        self.ps[4][pr, :], self.bmb[ri][:, p0 + g, :, :].rearrange("p a c -> p (a c)"),
                        wy[:, g, ri, 0:16, :], start=(ri == 0), stop=(ri == 1), tile_position=(0, 32 * g)),
                        r=["bmb%d" % ri, "wy%d" % ri], w=["ps4"])
            for g in range(4):
                pr = slice(32 * g, 32 * g + 32)
                self.cp(kw[pr, :, 32 * g:32 * g + 32], self.ps[4][pr, :].rearrange("p (t c) -> p t c", t=16),
                        r=["ps4"], w=["kw"], eng="act")
            uv = self.hT[:, ct, :].rearrange("p (n i) -> p n i", i=16)
            hres = ["hT.%d.%d" % (ct, t) for t in range(NTB)]
            for b in range(4):
                pv = self.ps[b][:, :].rearrange("p (n q) -> p n q", q=4)
                for tau in range(4 * b + 4):
                    q0 = max(0, tau - 4 * b)
                    self.mm(pv[:, :, q0:4], kw[:, tau, :], uv[:, :, 4 * b + q0 - tau:4 * b + 4 - tau],
                            tau == 0, False, r=["kw"] + hres, w=["ps%d" % b])
                for q in range(4):
                    i = 4 * b + q
                    for g in range(4):
                        pr = slice(32 * g, 32 * g + 32)
                        for ri in range(2):
                            last = (q == 3 and g == 3 and ri == 1)
                            S.op("pe", lambda e, g=g, ri=ri, pr=pr, i=i, q=q, b=b, last=last, p0=p0: e.matmul(
                                self.ps[b][pr, :].rearrange("p (n q) -> p n q", q=4)[:, :, q],
                                wy[:, g, ri, i + 1, :], sprev[:, p0 + g, ri, :],
                                start=False, stop=last, tile_position=(0, 32 * g)),
                                r=["wy%d" % ri, "sprev"], w=["ps%d" % b])
            for b in range(4):
                pv = self.ps[b][:, :].rearrange("p (n q) -> p n q", q=4)
                yt = ytmp[b % 2]
                self.stt(yt[:, :, :], uv[:, :, 4 * b:4 * b + 4], self.dcol[:, ct:ct + 1], pv, ALU.mult, ALU.add,
                         r=["ps%d" % b, "dcol"] + hres, w=["ytmp%d" % (b % 2)])
                self.act(uv[:, :, 4 * b:4 * b + 4], yt[:, :, :], AF.Gelu_apprx_tanh,
                         r=["ytmp%d" % (b % 2)], w=hres)

    def glu(self):
        wg = self.wglu[0].rearrange("(k p) f -> p k f", p=128)
        sg = [self.carve([128, TB], F32) for _ in range(2)]
        for gs in range(4):
            s = self.win_rr % 3
            self.win_rr += 1
            wsl = self.win_s[s]
            wn = "win%d" % s
            c0 = gs * 256
            self.dma(wsl[:, :, 0, :], wg[:, :, c0:c0 + 256], wn + "g", r=[], w=[wn + "g"], eng="pool")
            self.dma(wsl[:, :, 1, :], wg[:, :, D + c0:D + c0 + 256], wn + "u", r=[], w=[wn + "u"], eng="pool")
            for jl in range(2):
                m = 2 * gs + jl
                for t in range(NTB):
                    tb = slice(t * TB, (t + 1) * TB)
                    pa = 2 * (self.mm1_rr % 2)
                    self.mm1_rr += 1
                    for gu in range(2):
                        for k in range(NCH):
                            self.mm(self.ps[pa + gu][:, :], wsl[:, k, gu, jl * 128:(jl + 1) * 128],
                                    self.hT[:, k, tb], k == 0, k == NCH - 1,
                                    r=[wn + "gu"[gu], "hT.%d.%d" % (k, t)], w=["ps%d" % (pa + gu)])
                    sgt = sg[pa // 2]
                    sgn = "gsg%d" % (pa // 2)
                    self.act(sgt[:, :], self.ps[pa + 1][:, :], AF.Sigmoid, r=["ps%d" % (pa + 1)], w=[sgn])
                    self.tt(sgt[:, :], sgt[:, :], self.ps[pa][:, :], ALU.mult, r=[sgn, "ps%d" % pa], w=[sgn])
                    self.tt(self.xT[:, m, tb], self.xT[:, m, tb], sgt[:, :], ALU.add,
                            r=[sgn, "xT.%d.%d" % (m, t)], w=["xT.%d.%d" % (m, t)])

    def s5_mixer(self, n):
        self.phase("s5")
        self.hT = self.carve([128, NCH, NT], BF16)
        self.sq = self.carve([128, NCH, TB], BF16)
        self.rstd = [self.carve([128, TB], F32) for _ in range(2)]
        off_after_h = NCH * NT * 2
        self.rmsnorm(n)
        self.barrier()
        self.arena_off = off_after_h
        self.s5_tables()
        self.s5_pass_a()
        self.s5_exchange()
        if self.cfg.get("s5x") == "emit":
            return False
        self.s5_pass_b()
        self.barrier()
        self.arena_off = off_after_h
        self.glu()
        return True


    def rope_consts(self):
        S = self.S
        rowc = self.carve([1, 64], F32)
        self.ropec = self.carve([32, 2], F32)
        for d in range(32):
            j = d % 16
            self.memset(rowc[0:1, d:d + 1], float(np.float32(500000.0) ** np.float32(-(2.0 * j) / 32.0)), w=["rowc"])
        self.memset(rowc[0:1, 32:48], -1.0, w=["rowc"])
        self.memset(rowc[0:1, 48:64], 1.0, w=["rowc"])
        for i in range(2):
            self.mm(self.ps[7][0:32, 2 * i:2 * i + 2], rowc[0:1, 32 * i:32 * i + 32], self.ident[0:1, 0:2].bitcast(F32)
                    if False else rowc[0:1, 48:50], True, True, r=["rowc"], w=["ps7"])
        self.cp(self.ropec[:, :], self.ps[7][0:32, 0:4:2], r=["ps7"], w=["ropec"])
        self.pos_d = self.din["pos"].ap() if "pos" in self.din else self.dram_in("pos", [1, NT])

    def rope_tables(self, t0, n, bufs):
        posb, cosb, sinb, t1, ti = bufs
        rs = self.cfg.get("kv_skip", ())
        if "rope" in rs:
            self.memset(cosb[:, :], 1.0, w=["cosb"])
            self.memset(sinb[:, :], 0.0, w=["sinb"])
            return cosb, sinb
        self.dma(posb[:, :], self.pos_d[0:1, t0:t0 + n].partition_broadcast(32) if False else
                 self.pos_d[0, t0:t0 + n].partition_broadcast(32), "posb", r=[], w=["posb"])
        invf = self.ropec[:, 0:1]
        sgn = self.ropec[:, 1:2]
        self.ts(t1[:, :], posb[:, :], invf, 1.0 / (2 * np.pi), ALU.mult, ALU.mult, r=["posb", "ropec"], w=["rt1"])
        SC = 2 * np.pi * (1.0 - 1e-6)
        for (dst, off, nm) in ((sinb, 0.0, "sinb"), (cosb, 0.25, "cosb")):
            if off:
                self.ts(dst[:, :], t1[:, :], off, None, ALU.add, None, r=["rt1"], w=[nm])
            else:
                self.cp(dst[:, :], t1[:, :], r=["rt1"], w=[nm])
            self.cp(ti[:, :], dst[:, :], r=[nm], w=["rti"])
            self.cp(posb[:, :], ti[:, :], r=["rti"], w=["posb"])
            self.tt(dst[:, :], dst[:, :], posb[:, :], ALU.subtract, r=[nm, "posb"], w=[nm])
            self.act(dst[:, :], dst[:, :], AF.Sin, r=[nm], w=[nm], scale=SC)
        self.ts(sinb[:, :], sinb[:, :], sgn, None, ALU.mult, None, r=["sinb", "ropec"], w=["sinb"])
        return cosb, sinb

    def kv_phase(self):
        nc, S = self.nc, self.S
        self.phase("kv")
        self.hT = self.carve([128, NCH, NT], BF16)
        self.sq = self.carve([128, NCH, TB], BF16)
        self.rstd = [self.carve([128, TB], F32) for _ in range(2)]
        self.rmsnorm(8)
        emit = self.cfg.get("kvx") == "emit"
        kind = "ExternalOutput" if emit else "Internal"
        self.kT_loc_t = nc.dram_tensor("kT_loc", [128, 2 * NT], BF16, kind=kind)
        self.v_loc_t = nc.dram_tensor("v_loc", [128, 16 * 2 * 128], BF16, kind=kind)
        self.km_loc_t = nc.dram_tensor("km_loc", [128, 16], F32, kind=kind)
        self.kT_loc_d = self.kT_loc_t.ap().rearrange("p (h t) -> p h t", h=2)
        self.v_loc_d = self.v_loc_t.ap().rearrange("p (a d) -> p a d", a=32)
        self.km_loc_d = self.km_loc_t.ap().rearrange("p (h b) -> p h b", h=2)
        wk_d = self.dram_in("w_k", [D, 256]).rearrange("(k p) n -> p k n", p=128)
        wv_d = self.dram_in("w_v", [D, 256]).rearrange("(k p) n -> p k n", p=128)
        wk = self.wbuf[:, 0:2048].rearrange("p (k n) -> p k n", k=NCH)
        wv = self.wbuf[:, 2048:4096].rearrange("p (k n) -> p k n", k=NCH)
        wks = self.wbuf[:, 4096:4608].rearrange("p (k h n) -> p k h n", k=NCH, h=2)
        self.dma(wk, wk_d, "wk", r=[], w=["wk"], eng="pool")
        self.dma(wv, wv_d, "wv", r=[], w=["wv"], eng="pool")
        if "wks" in self.cfg.get("kv_skip", ()):
            self.memset(wks[:, :, :, :], 0.0, w=["wks0a", "wks0b", "wks1a", "wks1b"])
        for h in range(2 if "wks" not in self.cfg.get("kv_skip", ()) else 0):
            self.dma(wks[:, :, h, 0:16], wk_d[:, :, h * 128 + 16:h * 128 + 32], "wks%da" % h, r=[], w=["wks%da" % h], eng="pool")
            self.dma(wks[:, :, h, 16:32], wk_d[:, :, h * 128:h * 128 + 16], "wks%db" % h, r=[], w=["wks%db" % h], eng="pool")
        kT = self.carve([128, 2, NT], BF16)
        vl = self.carve([128, 16, 2, 130], BF16)
        km = self.carve([128, 2, 8], F32)
        self.rope_consts()
        rb = [self.carve([32, TB], F32) for _ in range(4)] + [self.carve([32, TB], I32)]
        tq = [self.carve([32, TB], F32) for _ in range(2)]
        self.memset(vl[:, :, :, 128:130], 1.0, w=["vl1"])
        cnt = 0
        kvs = self.cfg.get("kv_skip", ())
        if "k" in kvs:
            self.memset(kT[:, :, :], 0.0, w=["kT.%d.%d" % (h, t) for h in range(2) for t in range(NTB)])
        for t in range(NTB if "k" not in kvs else 0):
            tb = slice(t * TB, (t + 1) * TB)
            cosb, sinb = self.rope_tables(t * TB, TB, rb)
            for h in range(2):
                pk, psw = cnt % 2, 2 + cnt % 2
                cnt += 1
                for k in range(NCH):
                    self.mm(self.ps[pk][:, :], wk[:, k, h * 128:(h + 1) * 128], self.hT[:, k, tb], k == 0, k == NCH - 1,
                            r=["wk", "hT.%d.%d" % (k, t)], w=["ps%d" % pk])
                for k in range(NCH):
                    self.mm(self.ps[psw][0:32, :], wks[:, k, h, :], self.hT[:, k, tb], k == 0, k == NCH - 1,
                            r=["wks%da" % h, "wks%db" % h, "hT.%d.%d" % (k, t)], w=["ps%d" % psw])
                self.cp(kT[:, h, tb], self.ps[pk][:, :], r=["ps%d" % pk], w=["kT.%d.%d" % (h, t)], eng="act")
                if "krope" in kvs:
                    continue
                self.tt(tq[0][:, :], self.ps[pk][0:32, :], cosb[:, :], ALU.mult, r=["ps%d" % pk, "cosb"], w=["tq0"])
                self.tt(tq[1][:, :], self.ps[psw][0:32, :], sinb[:, :], ALU.mult, r=["ps%d" % psw, "sinb"], w=["tq1"])
                self.tt(kT[0:32, h, tb], tq[0][:, :], tq[1][:, :], ALU.add, r=["tq0", "tq1"], w=["kT.%d.%d" % (h, t)])
        if "v" in kvs:
            self.memset(vl[:, :, :, 0:128], 0.0, w=["vl.%d" % i for i in range(16)])
        for tt_ in range(16 if "v" not in kvs else 0):
            pv = 4 + tt_ % 2
            for k in range(NCH):
                self.mm(self.ps[pv][:, 0:256], self.hT[:, k, tt_ * 128:(tt_ + 1) * 128], wv[:, k, :], k == 0, k == NCH - 1,
                        r=["wv", "hT.%d.%d" % (k, tt_ // 4)], w=["ps%d" % pv])
            self.cp(vl[:, tt_, :, 0:128], self.ps[pv][:, 0:256].rearrange("p (h d) -> p h d", h=2),
                    r=["ps%d" % pv], w=["vl.%d" % tt_], eng=("act" if tt_ % 2 else "dve"))
        allk = ["kT.%d.%d" % (h, t) for h in range(2) for t in range(NTB)]
        if "km" in kvs:
            self.memset(km[:, :, :], 0.0, w=["km"])
        for h in range(2 if "km" not in kvs else 0):
            S.op("dve", lambda e, h=h: e.tensor_reduce(km[:, h, :], kT[:, h, :].rearrange("p (b n) -> p b n", b=8),
                                                      mybir.AxisListType.X, ALU.add), r=allk, w=["km"])
        self.ts(km[:, :, :], km[:, :, :], 1.0 / 256.0, None, ALU.mult, None, r=["km"], w=["km"])
        self.dma(self.kT_loc_d, kT[:, :, :], "kTd", r=allk, w=["kTd"])
        self.dma(self.v_loc_d, vl[:, :, :, 0:128].rearrange("p a h d -> p (a h) d"), "vld",
                 r=["vl1"] + ["vl.%d" % i for i in range(16)], w=["vld"])
        self.dma(self.km_loc_d, km[:, :, :], "kmd", r=["km"], w=["kmd"])
        S.op("sp", None, r=["kTd", "vld", "kmd"], w=[])
        if self.cfg.get("kvx") == "cc":
            self.kT_g = nc.dram_tensor("kT_gath", [512, 2 * NT], BF16)
            self.v_g = nc.dram_tensor("v_gath", [512, 16 * 2 * 128], BF16)
            self.km_g = nc.dram_tensor("km_gath", [512, 16], F32)
            self.allgather(self.kT_g, self.kT_loc_t, "cc_k", r=["kTd"], w=["kTg"])
            self.allgather(self.v_g, self.v_loc_t, "cc_v", r=["vld"], w=["vg"])
            self.allgather(self.km_g, self.km_loc_t, "cc_m", r=["kmd"], w=["kmg"])

    def attn_phase(self, n):
        nc, S = self.nc, self.S
        self.phase("attn")
        QB = 256
        cc = self.cfg.get("kvx") == "cc"
        if not cc:
            ktall_d = self.dram_in("kT_all", [128, 2, 32, 256], BF16)
            vall_d = self.dram_in("v_all", [128, 64, 2, 130], BF16)
            kmall_d = self.dram_in("km_all", [128, 2, 32])
        vbias_d = self.dram_in("vbias", [8, 32])
        wq_d = self.dram_in("w_q", [1, D, D])[0].rearrange("(k p) n -> p k n", p=128)
        wo_d = self.dram_in("w_o", [1, D, D])[0].rearrange("(h p) n -> p h n", p=128)
        wq = self.wbuf[:, 0:8192].rearrange("p (k n) -> p k n", k=NCH)
        wo = self.wbuf[:, 8192:16384].rearrange("p (h n) -> p h n", h=8)
        wqs = self.wbuf[:, 16384:18432].rearrange("p (k h n) -> p k h n", k=NCH, h=8)
        self.dma(wq, wq_d, "wq", r=[], w=["wq"], eng="pool")
        self.dma(wo, wo_d, "wo", r=[], w=["wo"], eng="pool")
        wqv = wq_d.rearrange("p k (h n) -> p k h n", h=8)
        for k in range(NCH):
            self.dma(wqs[:, k, :, 0:16], wqv[:, k, :, 16:32], "wqs", r=[], w=["wqsa%d" % k], eng="pool")
            self.dma(wqs[:, k, :, 16:32], wqv[:, k, :, 0:16], "wqs", r=[], w=["wqsb%d" % k], eng="pool")
        KT = self.carve([128, 2, 32, 256], BF16)
        VA = self.carve([128, 64, 2, 130], BF16)
        kmf = self.carve([128, 2, 32], F32)
        kmb = self.carve([128, 2, 32], BF16)
        vb = self.carve([128, 8, 32], F32)
        self.KTres, self.VAres = ["KT"], ["VA"]
        if not cc:
            self.dma(KT[:, :, :, :], ktall_d, "KT", r=[], w=["KT"])
            self.dma(VA[:, :, :, :], vall_d, "VA", r=[], w=["VA"])
            self.dma(kmf[:, :, :], kmall_d, "kmf", r=[], w=["kmf"])
        else:
            kg_ = self.kT_g.ap().rearrange("(q p) (h i n) -> q p h i n", q=4, h=2, i=8)
            vg_ = self.v_g.ap().rearrange("(q p) (i a d) -> q p i a d", q=4, i=8, a=4)
            mg_ = self.km_g.ap().rearrange("(q p) (h i) -> q p h i", q=4, h=2)
            VAv = VA[:, :, :, :].rearrange("p (m a) h d -> p m (a h d)", m=4)
            self.top8_ = self.carve([128, 8, 8], F32)
            kmg = self.top8_[:, :, :].rearrange("p a b -> p (a b)").rearrange("p (q h i) -> p q h i", q=4, h=2)
            KTr, VAr = [], []
            for q in range(4):
                for par in range(2):
                    g0 = q if par == 0 else 7 - q
                    for h in range(2):
                        nm = "KT.%d.%d.%d" % (q, par, h)
                        self.dma(KT[:, h, g0::8, :], kg_[q][:, h, par::2, :], "KTall", r=["kTg"], w=[nm])
                        KTr.append(nm)
                    for m in range(4):
                        nm = "VA.%d.%d.%d" % (q, par, m)
                        gblk = 8 * m + g0
                        self.dma(VA[:, 2 * gblk:2 * gblk + 2, :, 0:128].rearrange("p a h d -> p (a h) d"),
                                 vg_[q][:, 2 * m + par, :, :], "VAall", r=["vg", "vones"], w=[nm])
                        VAr.append(nm)
                self.dma(kmg[:, q, :, :], mg_[q], "kmg%d" % q, r=["kmg"], w=["kmgs%d" % q])
                self.cp(kmf[:, :, q::8], kmg[:, q, :, 0::2], r=["kmgs%d" % q], w=["kmf"])
                self.cp(kmf[:, :, 7 - q::8], kmg[:, q, :, 1::2], r=["kmgs%d" % q], w=["kmf"])
            self.KTres, self.VAres = KTr, VAr
        self.cp(kmb[:, :, :], kmf[:, :, :], r=["kmf"], w=["kmb"])
        self.dma(vb[:, :, :], vbias_d.partition_broadcast(128), "vb", r=[], w=["vb"])
        self.rope_consts()
        self.memset(VA[:, :, :, 128:130], 1.0, w=["vones"])
        hq = self.carve([128, NCH, QB], BF16)
        qT = self.carve([128, 8, QB], BF16)
        sqb = self.carve([128, NCH, QB], BF16)
        rsb = self.carve([128, QB], F32)
        rb = [self.carve([32, QB], F32) for _ in range(4)] + [self.carve([32, QB], I32)]
        tq = [self.carve([32, QB], F32) for _ in range(2)]
        kown = [self.carve([128, 2, 256], BF16) for _ in range(2)]
        vown = [self.carve([128, 2, 2, 130], BF16) for _ in range(2)]
        pT = [self.carve([128, 2, 512], BF16) for _ in range(2)]
        acc = self.carve([128, 8, 130], F32)
        gsb = self.carve([128, 8, 32], F32)
        sel = self.carve([128, 8, 32], F32)
        top8 = self.top8_ if cc else self.carve([128, 8, 8], F32)
        thr = self.carve([128, 8], F32)
        rinv = self.carve([128, 8], F32)
        otm = self.carve([128, 8, 128], BF16)
        oT = self.carve([128, 8, 128], BF16)
        tri = self.carve([128, 128], BF16)
        for vv_ in vown:
            self.memset(vv_[:, :, :, 128:130], 1.0, w=["vones"])
        self.ts(tri[:, :], self.iot[:, :], 0.0, None, ALU.is_ge, None, r=["iot"], w=["tri"])
        g = self.gcol
        QSC = float(128.0 ** -0.5)
        psb7 = self.ps[7][:, :].bitcast(BF16)
        npm_tab = [3, 7, 11, 15, 19, 23, 27, 31]
        for lb in range(8):
            t0 = lb * QB
            ts_ = slice(t0, t0 + QB)
            tix = t0 // TB
            ko, vo = kown[lb % 2], vown[lb % 2]
            kon, von = "kown%d" % (lb % 2), "vown%d" % (lb % 2)
            self.dma(ko[:, :, :], self.kT_loc_d[:, :, ts_], kon, r=["kTd"], w=[kon])
            self.dma(vo[:, :, :, 0:128].rearrange("p a h d -> p (a h) d"), self.v_loc_d[:, 4 * lb:4 * lb + 4, :],
                     von, r=["vld", "vones"], w=[von])
            for c in range(NCH):
                self.act(sqb[:, c, :], self.xT[:, c, ts_], AF.Square, r=["xT.%d.%d" % (c, tix)], w=["sqb.%d" % c])
            for c in range(NCH):
                self.mm(self.ps[6][:, 0:QB], self.onesm[:, :], sqb[:, c, :], c == 0, c == NCH - 1,
                        r=["onesm", "sqb.%d" % c], w=["ps6"])
            self.ts(rsb[:, :], self.ps[6][:, 0:QB], EPS, None, ALU.add, None, r=["ps6"], w=["rsb"])
            self.act(rsb[:, :], rsb[:, :], AF.Sqrt, r=["rsb"], w=["rsb"])
            S.op("dve", lambda e: e.reciprocal(rsb[:, :], rsb[:, :]), r=["rsb"], w=["rsb"])
            for c in range(NCH):
                self.stt(hq[:, c, :], self.xT[:, c, ts_], g[:, n, c:c + 1], rsb[:, :], ALU.mult, ALU.mult,
                         r=["xT.%d.%d" % (c, tix), "gcol", "rsb"], w=["hq.%d" % c])
            hqr = ["hq.%d" % c for c in range(NCH)]
            cosb, sinb = self.rope_tables(t0, QB, rb)
            for h in range(8):
                pq = h % 2
                for k in range(NCH):
                    self.mm(self.ps[pq][:, 0:QB], wq[:, k, h * 128:(h + 1) * 128], hq[:, k, :], k == 0, k == NCH - 1,
                            r=["wq", "hq.%d" % k], w=["ps%d" % pq])
                for k in range(NCH):
                    self.mm(self.ps[2 + pq][0:32, 0:QB], wqs[:, k, h, :], hq[:, k, :], k == 0, k == NCH - 1,
                            r=["wqsa%d" % kk for kk in range(NCH)] + ["wqsb%d" % kk for kk in range(NCH)] + ["hq.%d" % k],
                            w=["ps%d" % (2 + pq)])
                self.act(qT[:, h, :], self.ps[pq][:, 0:QB], AF.Copy, r=["ps%d" % pq], w=["qT.%d" % h], scale=QSC)
                self.stt(tq[0][:, :], self.ps[pq][0:32, 0:QB], QSC, cosb[:, :], ALU.mult, ALU.mult,
                         r=["ps%d" % pq, "cosb"], w=["tq0"])
                self.stt(tq[1][:, :], self.ps[2 + pq][0:32, 0:QB], QSC, sinb[:, :], ALU.mult, ALU.mult,
                         r=["ps%d" % (2 + pq), "sinb"], w=["tq1"])
                self.tt(qT[0:32, h, :], tq[0][:, :], tq[1][:, :], ALU.add, r=["tq0", "tq1"], w=["qT.%d" % h])
            qres = ["qT.%d" % h for h in range(8)]
            npm = npm_tab[lb]
            for hf in range(2):
                qs = slice(hf * 128, hf * 128 + 128)
                xs = slice(t0 + hf * 128, t0 + hf * 128 + 128)
                for h in range(8):
                    self.mm(self.ps[6][:, h * 32:(h + 1) * 32], qT[:, h, qs], kmb[:, h // 4, :], True, True,
                            r=["qT.%d" % h, "kmb"], w=["ps6"])
                self.tt(gsb[:, :, :], self.ps[6][:, 0:256].rearrange("p (h n) -> p h n", h=8),
                        self.bc(vb[:, lb, :], 1, 8), ALU.add, r=["ps6", "vb"], w=["gsb"])
                for h in range(8):
                    S.op("dve", lambda e, h=h: e.max(top8[:, h, :], gsb[:, h, :]), r=["gsb"], w=["top8"])
                self.ts(thr[:, :], top8[:, :, 2], -1e29, None, ALU.max, None, r=["top8"], w=["thr"])
                self.tt(sel[:, :, :], gsb[:, :, :], self.bc(thr[:, :], 2, 32), ALU.is_ge, r=["gsb", "thr"], w=["sel"])
                units = [(kg, nb) for kg in range(2) for nb in [-1] + list(range(npm))]

                def stage_a(u):
                    kg, nb = units[u]
                    hs = slice(4 * kg, 4 * kg + 4)
                    pi = (self.attn_rr + u) % 2
                    pS = (0 + 2 * pi, 1 + 2 * pi)
                    pt = pT[pi]
                    ptn = "pT%d" % pi
                    halves = [0] if (nb < 0 and hf == 0) else [0, 1]
                    for kh in halves:
                        if nb < 0:
                            lhs = ko[:, kg, kh * 128:(kh + 1) * 128]
                            rr = [kon]
                        else:
                            lhs = KT[:, kg, nb, kh * 128:(kh + 1) * 128]
                            rr = self.KTres
                        self.mm(self.ps[pS[kh]][:, :], lhs, qT[:, hs, qs], True, True,
                                r=rr + qres[4 * kg:4 * kg + 4], w=["ps%d" % pS[kh]])
                        self.act(pt[:, kh, :], self.ps[pS[kh]][:, :], AF.Exp, r=["ps%d" % pS[kh]], w=[ptn + ".%d" % kh])
                    if nb < 0:
                        khd = hf
                        self.tt(pt[:, khd, :].rearrange("p (h t) -> p h t", h=4),
                                pt[:, khd, :].rearrange("p (h t) -> p h t", h=4),
                                self.bc(tri[:, :], 1, 4), ALU.mult, r=[ptn + ".%d" % khd, "tri"], w=[ptn + ".%d" % khd])

                def stage_b(u):
                    kg, nb = units[u]
                    pi = (self.attn_rr + u) % 2
                    pO = (4 + 2 * pi, 5 + 2 * pi)
                    pt = pT[pi]
                    ptn = "pT%d" % pi
                    halves = [0] if (nb < 0 and hf == 0) else [0, 1]
                    for hh in range(4):
                        po = pO[hh // 2]
                        oview = self.ps[po][:, (hh % 2) * 130:(hh % 2) * 130 + 130]
                        for kh in halves:
                            if nb < 0:
                                rhs = vo[:, kh, kg, :]
                                rr = [von]
                            else:
                                rhs = VA[:, 2 * nb + kh, kg, :]
                                rr = self.VAres
                            self.mm(oview, pt[:, kh, hh * 128:(hh + 1) * 128], rhs, kh == halves[0], kh == halves[-1],
                                    r=rr + [ptn + ".%d" % kh], w=["ps%d" % po])
                    for hh in range(4):
                        h = 4 * kg + hh
                        po = pO[hh // 2]
                        oview = self.ps[po][:, (hh % 2) * 130:(hh % 2) * 130 + 130]
                        if nb < 0:
                            self.cp(acc[:, h, :], oview, r=["ps%d" % po], w=["acc.%d" % h])
                        else:
                            self.stt(acc[:, h, :], oview, sel[:, h, nb:nb + 1], acc[:, h, :], ALU.mult, ALU.add,
                                     r=["ps%d" % po, "sel", "acc.%d" % h], w=["acc.%d" % h])

                nu = len(units)
                stage_a(0)
                for u in range(nu):
                    if u + 1 < nu:
                        stage_a(u + 1)
                    stage_b(u)
                self.attn_rr += nu
                accr = ["acc.%d" % h for h in range(8)]
                S.op("dve", lambda e: e.reciprocal(rinv[:, :], acc[:, :, 128]), r=accr, w=["rinv"])
                for h in range(8):
                    self.ts(otm[:, h, :], acc[:, h, 0:128], rinv[:, h:h + 1], None, ALU.mult, None,
                            r=["acc.%d" % h, "rinv"], w=["otm"])
                for h in range(8):
                    self.tr(psb7[:, h * 128:(h + 1) * 128], otm[:, h, :], self.identb2[:, :], r=["otm", "identb2"], w=["ps7"])
                self.cp(oT[:, :, :], psb7[:, 0:1024].rearrange("p (h t) -> p h t", h=8), r=["ps7"], w=["oT"], eng="act")
                for cg in range(2):
                    pw = self.ps[6] if cg == 0 else self.ps[7]
                    pwn = "ps6" if cg == 0 else "ps7"
                    for cc in range(4):
                        c = 4 * cg + cc
                        for h in range(8):
                            self.mm(pw[:, cc * 128:(cc + 1) * 128], wo[:, h, c * 128:(c + 1) * 128], oT[:, h, :],
                                    h == 0, h == 7, r=["wo", "oT"], w=[pwn])
                    self.tt(self.xT[:, 4 * cg:4 * cg + 4, xs], self.xT[:, 4 * cg:4 * cg + 4, xs],
                            pw[:, :].rearrange("p (c t) -> p c t", c=4), ALU.add,
                            r=[pwn] + ["xT.%d.%d" % (4 * cg + cc, tix) for cc in range(4)],
                            w=["xT.%d.%d" % (4 * cg + cc, tix) for cc in range(4)])


def build(cfg):
    b = Builder(cfg)
    b.setup_common()
    b.load_x()
    stages = cfg.get("stages", 99)
    cont = True
    skip = cfg.get("skip", ())
    if stages >= 1 and 1 not in skip:
        b.ffn(0, 0, 0)
    if stages >= 2 and 2 not in skip:
        cont = b.s5_mixer(1)
    if cont and stages >= 3 and 3 not in skip:
        b.ffn(0, 1, 2)
    if cont and stages >= 4:
        b.kv_phase()
        if cfg.get("kvx") == "emit":
            cont = False
    if cont and stages >= 5:
        b.ffn(1, 0, 3)
    if cont and stages >= 6:
        b.attn_phase(4)
    if cont and stages >= 7:
        b.ffn(1, 1, 5)
    if cont or cfg.get("store_anyway"):
        b.store_x(final_norm=cfg.get("final_norm", False))
    if b.dbg_keys:
        b.S.op("sp", None, r=b.dbg_keys, w=[])
    b.S.emit()
    return b


def shard_tokens(x):
    out = []
    for core in range(N_CORES):
        bb, r = core // 4, core % 4
        blocks = [zz_block(r, i) for i in range(8)]
        out.append(np.ascontiguousarray(
            np.concatenate([x[bb, g * 256:(g + 1) * 256, :] for g in blocks], axis=0)))
    return out


def unshard_tokens(parts):
    full = np.zeros((2, 8192, D), dtype=parts[0].dtype)
    for core in range(N_CORES):
        bb, r = core // 4, core % 4
        for i in range(8):
            g = zz_block(r, i)
            full[bb, g * 256:(g + 1) * 256, :] = parts[core][i * 256:(i + 1) * 256, :]
    return full


def core_meta(core):
    bb, r = core // 4, core % 4
    blocks = [zz_block(r, i) for i in range(8)]
    pos = np.concatenate([np.arange(g * 256, (g + 1) * 256) for g in blocks]).astype(np.float32)[None, :]
    sel = np.zeros((8, 32), np.float32)
    vbias = np.zeros((8, 32), np.float32)
    for i, g in enumerate(blocks):
        sel[i, g] = 1.0
        vbias[i, g:] = -1e30
    return {"pos": pos, "blk_sel": sel, "vbias": vbias}


def run(cfg, inputs, extra=None, trace=False):
    b = build(cfg)
    xs = shard_tokens(np.asarray(inputs["x"], dtype=np.float32))
    in_maps = []
    for core in range(N_CORES):
        m = {}
        meta = core_meta(core)
        for name in b.din:
            if name == "x":
                m[name] = xs[core]
            elif name in meta:
                m[name] = meta[name]
            elif extra is not None and name in extra[core]:
                m[name] = extra[core][name]
            else:
                m[name] = np.ascontiguousarray(np.asarray(inputs[name], dtype=np.float32))
        in_maps.append(m)
    res = run_bass_kernel_spmd(b.nc, in_maps, core_ids=list(range(N_CORES)), trace=trace)
    return res


def relay_f(results):
    out = []
    for bb in range(2):
        fa = np.zeros((128, 32, 2, 32), np.float32)
        for r in range(4):
            f = np.asarray(results[4 * bb + r]["f_out"])
            for i in range(8):
                fa[:, :, :, zz_block(r, i)] = f[:, :, :, i]
        out.append(fa)
    return [{"f_all": out[c // 4]} for c in range(N_CORES)]


def relay_kv(results, extra):
    for bb in range(2):
        r0 = results[4 * bb]
        kt = np.zeros((128, 2, 32, 256), np.asarray(r0["kT_loc"]).dtype)
        va = np.zeros((128, 64, 2, 130), np.asarray(r0["v_loc"]).dtype)
        km = np.zeros((128, 2, 32), np.float32)
        for r in range(4):
            res = results[4 * bb + r]
            k = np.asarray(res["kT_loc"]).reshape(128, 2, 8, 256)
            v = np.asarray(res["v_loc"])
            m = np.asarray(res["km_loc"])
            for i in range(8):
                g = zz_block(r, i)
                kt[:, :, g, :] = k[:, :, i, :]
                va[:, 2 * g:2 * g + 2] = v[:, 2 * i:2 * i + 2]
                km[:, :, g] = m[:, :, i]
        for r in range(4):
            extra[4 * bb + r].update({"kT_all": kt, "v_all": va, "km_all": km})
    return extra


def gather_out(res):
    return unshard_tokens([np.asarray(r["out"]) for r in res.results])


def run_all(inputs, final_norm=True, stages=7):
    ra = run({"stages": 2, "s5x": "emit"}, inputs)
    extra = relay_f(ra.results)
    rb = run({"stages": 4, "s5x": "input", "kvx": "emit"}, inputs, extra=extra)
    extra = relay_kv(rb.results, extra)
    rc = run({"stages": stages, "s5x": "input", "kvx": "input", "final_norm": final_norm}, inputs, extra=extra)
    return gather_out(rc), (ra, rb, rc)


def kernel(**inputs):
    res = run({"stages": 7, "s5x": "cc", "kvx": "cc", "final_norm": True}, inputs)
    return gather_out(res).astype(np.float32)
```
